# Optimizing a Trainium2 kernel written in Bass

```python
import jax, jax.numpy as jnp
from jax import lax
import numpy as np

D_MODEL = 1024
BATCH = 32
SEQ = 256
DEPTH = 2
DEC_BATCH = 8
DEC_SEQ = 1024
PAST_LEN = 256

GRID_W = 64
BRANCH_WIDTH = D_MODEL // 2
N_BRANCH = 3
NA_HEADS = 8
NA_HEAD_DIM = BRANCH_WIDTH // NA_HEADS
NA_WIDTH = NA_HEADS * NA_HEAD_DIM
NA_WIN_H_MAX = 8
NA_WIN_W = 16
NA_QCOL_BLOCK = 16
NA_KCOL_BLOCK = NA_QCOL_BLOCK + NA_WIN_W
SGU_GROUPS = 4
SGU_CHUNK = 128
SGU_WIDTH = BRANCH_WIDTH
SGU_GROUP_DIM = SGU_WIDTH // SGU_GROUPS
MLA_HEADS = 8
MLA_NOPE = 64
MLA_ROPE = 32
MLA_V = BRANCH_WIDTH // MLA_HEADS
MLA_WIDTH = MLA_HEADS * MLA_V
MLA_Q_LORA = 384
MLA_KV_LORA = 256
ROPE_THETA = 10000.0
Q_BLOCK = 128
IN_COLS = 4 * NA_WIDTH + 3 * SGU_WIDTH + MLA_Q_LORA + MLA_KV_LORA + MLA_ROPE + MLA_WIDTH + N_BRANCH * D_MODEL
EPS = 1e-6
NEG_INF = -1e30

kernel_name = 'hybrid_diffusion_na_sgu_mla_step'


def _rmsnorm(x, g):
    xf = x.astype(jnp.float32)
    y = xf * lax.rsqrt(jnp.mean(xf * xf, axis=-1, keepdims=True) + EPS)
    return (y * g.astype(jnp.float32)).astype(x.dtype)


def _layernorm(x):
    xf = x.astype(jnp.float32)
    mu = jnp.mean(xf, axis=-1, keepdims=True)
    var = jnp.mean(jnp.square(xf - mu), axis=-1, keepdims=True)
    return ((xf - mu) * lax.rsqrt(var + EPS)).astype(x.dtype)


def _modulation(cond, w_mod, b_mod):
    m = jnp.einsum('...d,de->...e', jax.nn.silu(cond), w_mod) + b_mod
    return jnp.split(m, 3, axis=-1)


def _split_cols(p):
    sizes = (NA_WIDTH,) * 4 + (SGU_WIDTH,) * 3 + (MLA_Q_LORA, MLA_KV_LORA, MLA_ROPE, MLA_WIDTH, N_BRANCH * D_MODEL)
    return jnp.split(p, np.cumsum(sizes)[:-1].tolist(), axis=-1)


def _pre_mixer(x, shift, scale, norm_g, w_in):
    h = _rmsnorm(x, norm_g) * (1 + scale) + shift
    return _split_cols(jnp.einsum('bnd,de->bne', h, w_in))


def _axial_rope(n_tokens):
    pos = jnp.arange(n_tokens, dtype=jnp.int32)
    row = (pos // GRID_W).astype(jnp.float32)
    col = (pos % GRID_W).astype(jnp.float32)
    n_freq = MLA_ROPE // 4
    inv = ROPE_THETA ** (-jnp.arange(n_freq, dtype=jnp.float32) / n_freq)
    ang = jnp.concatenate([row[:, None] * inv, col[:, None] * inv], axis=-1)
    return jnp.cos(ang), jnp.sin(ang)


def _rope(x, cos, sin):
    x1, x2 = jnp.split(x.astype(jnp.float32), 2, axis=-1)
    return jnp.concatenate([x1 * cos - x2 * sin, x1 * sin + x2 * cos], axis=-1).astype(x.dtype)


def _over_query_blocks(fn, q):
    B, N = q.shape[:2]
    qb = jnp.moveaxis(q.reshape((B, N // Q_BLOCK, Q_BLOCK) + q.shape[2:]), 1, 0)
    o = lax.map(fn, qb)
    return jnp.moveaxis(o, 0, 1).reshape((B, N) + o.shape[3:])


def _context_attention(q, k, v):
    scale = q.shape[-1] ** -0.5

    def block(qi):
        s = jnp.einsum('bqhd,bmhd->bhqm', qi, k).astype(jnp.float32) * scale
        p = jax.nn.softmax(s, axis=-1).astype(v.dtype)
        return jnp.einsum('bhqm,bmhd->bqhd', p, v)

    o = _over_query_blocks(block, q)
    return o.reshape(o.shape[0], o.shape[1], -1)


def _na_latent_attention(q, k, v, k_ctx, v_ctx, rpb):
    B, N, H, dh = q.shape
    rows = N // GRID_W
    wh = min(NA_WIN_H_MAX, rows)
    n_cb = GRID_W // NA_QCOL_BLOCK
    r = np.arange(rows)
    row_idx = np.clip(r - wh // 2, 0, rows - wh)[:, None] + np.arange(wh)[None, :]
    qcol = np.arange(GRID_W).reshape(n_cb, NA_QCOL_BLOCK)
    col_idx = np.clip(qcol[:, 0] - NA_WIN_W // 2, 0, GRID_W - NA_KCOL_BLOCK)[:, None] + np.arange(NA_KCOL_BLOCK)[None, :]
    win_lo = np.clip(qcol - NA_WIN_W // 2, 0, GRID_W - NA_WIN_W)
    in_win = (col_idx[:, None, :] >= win_lo[:, :, None]) & (col_idx[:, None, :] < win_lo[:, :, None] + NA_WIN_W)
    rel_r = row_idx - r[:, None] + NA_WIN_H_MAX - 1
    rel_c = np.clip(col_idx[:, None, :] - qcol[:, :, None] + NA_WIN_W - 1, 0, 2 * NA_WIN_W - 2)
    bias = rpb.astype(jnp.float32)[:, rel_r[:, None, None, :, None], rel_c[None, :, :, None, :]]
    bias = jnp.where(in_win[None, None, :, :, None, :], bias, NEG_INF)
    qb = q.reshape(B, rows, n_cb, NA_QCOL_BLOCK, H, dh)
    gather_r = row_idx[:, :, None, None]
    gather_c = col_idx[None, None, :, :]
    kg = k.reshape(B, rows, GRID_W, H, dh)[:, gather_r, gather_c]
    vg = v.reshape(B, rows, GRID_W, H, dh)[:, gather_r, gather_c]
    scale = dh ** -0.5
    s_loc = jnp.einsum('brjqhd,brwjkhd->bhrjqwk', qb, kg).astype(jnp.float32) * scale + bias
    s_ctx = jnp.einsum('brjqhd,bmhd->bhrjqm', qb, k_ctx).astype(jnp.float32) * scale
    n_loc = wh * NA_KCOL_BLOCK
    s = jnp.concatenate([s_loc.reshape(B, H, rows, n_cb, NA_QCOL_BLOCK, n_loc), s_ctx], axis=-1)
    p = jax.nn.softmax(s, axis=-1).astype(v.dtype)
    p_loc = p[..., :n_loc].reshape(B, H, rows, n_cb, NA_QCOL_BLOCK, wh, NA_KCOL_BLOCK)
    p_ctx = p[..., n_loc:]
    o = jnp.einsum('bhrjqwk,brwjkhd->brjqhd', p_loc, vg) + jnp.einsum('bhrjqm,bmhd->brjqhd', p_ctx, v_ctx)
    return o.reshape(B, N, H * dh)


def _sgu(u, v, w_s, b_s):
    B, N, _ = u.shape
    nc = N // SGU_CHUNK
    u = jax.nn.gelu(u)
    v = _layernorm(jax.nn.gelu(v))
    vg = v.reshape(B, nc, SGU_CHUNK, SGU_GROUPS, SGU_GROUP_DIM)
    mixed = jnp.einsum('gpq,bnqgc->bnpgc', w_s, vg) + b_s.T[:, :, None]
    return u * mixed.reshape(B, N, SGU_WIDTH)


def _mla_q(dq, q_norm, w_uq):
    B, N, _ = dq.shape
    q = jnp.einsum('bnr,re->bne', _rmsnorm(dq, q_norm), w_uq)
    return q.reshape(B, N, MLA_HEADS, MLA_NOPE + MLA_ROPE)


def _mla_expand(c_kv, w_ukv):
    B, L, _ = c_kv.shape
    kv = jnp.einsum('blr,re->ble', c_kv, w_ukv).reshape(B, L, MLA_HEADS, MLA_NOPE + MLA_V)
    return kv[..., :MLA_NOPE], kv[..., MLA_NOPE:]


def _mla_attention(q, k_nope, k_rope, v):
    scale = (MLA_NOPE + MLA_ROPE) ** -0.5

    def block(qi):
        s = (jnp.einsum('bqhd,bmhd->bhqm', qi[..., :MLA_NOPE], k_nope)
             + jnp.einsum('bqhr,bmr->bhqm', qi[..., MLA_NOPE:], k_rope)).astype(jnp.float32) * scale
        p = jax.nn.softmax(s, axis=-1).astype(v.dtype)
        return jnp.einsum('bhqm,bmhd->bqhd', p, v)

    o = _over_query_blocks(block, q)
    return o.reshape(o.shape[0], o.shape[1], MLA_WIDTH)


def _merge(ys, zs, merge_logits, w_branch, w_out):
    B, N, _ = merge_logits.shape
    gated = jnp.stack([y * jax.nn.silu(z) for y, z in zip(ys, zs)], axis=2)
    branch = jnp.einsum('bnkw,kwd->bnkd', gated, w_branch)
    gates = jax.nn.sigmoid(merge_logits).reshape(B, N, N_BRANCH, D_MODEL)
    merged = jnp.sum(gates * branch, axis=2)
    return jnp.einsum('bnd,de->bne', merged, w_out)


def _context_layer(x, c_ctx, lp):
    norm_g, w_mod, b_mod, w_in, na_rpb, sgu_w, sgu_b, q_norm, w_uq, kv_norm, w_ukv, w_branch, w_out = lp
    shift, scale, gate = _modulation(c_ctx, w_mod, b_mod)
    na_q, na_k, na_v, na_z, su, sv, sz, dq, dkv, kr, mz, mg = _pre_mixer(x, shift, scale, norm_g, w_in)
    B, L, _ = x.shape
    hs = (B, L, NA_HEADS, NA_HEAD_DIM)
    k_c, v_c = na_k.reshape(hs), na_v.reshape(hs)
    y_na = _context_attention(na_q.reshape(hs), k_c, v_c)
    y_sgu = _sgu(su, sv, sgu_w, sgu_b)
    q = _mla_q(dq, q_norm, w_uq)
    ckv = _rmsnorm(dkv, kv_norm)
    k_nope, v = _mla_expand(ckv, w_ukv)
    y_mla = _mla_attention(q, k_nope, kr, v)
    out = _merge((y_na, y_sgu, y_mla), (na_z, sz, mz), mg, w_branch, w_out)
    return x + gate * out, k_c, v_c, ckv, kr


def _latent_layer(x, c, k_ctx, v_ctx, ckv_ctx, kr_ctx, cos, sin, lp):
    norm_g, w_mod, b_mod, w_in, na_rpb, sgu_w, sgu_b, q_norm, w_uq, kv_norm, w_ukv, w_branch, w_out = lp
    shift, scale, gate = (m[:, None, :] for m in _modulation(c, w_mod, b_mod))
    na_q, na_k, na_v, na_z, su, sv, sz, dq, dkv, kr, mz, mg = _pre_mixer(x, shift, scale, norm_g, w_in)
    B, N, _ = x.shape
    hs = (B, N, NA_HEADS, NA_HEAD_DIM)
    y_na = _na_latent_attention(na_q.reshape(hs), na_k.reshape(hs), na_v.reshape(hs), k_ctx, v_ctx, na_rpb)
    y_sgu = _sgu(su, sv, sgu_w, sgu_b)
    q = _mla_q(dq, q_norm, w_uq)
    q = jnp.concatenate([q[..., :MLA_NOPE], _rope(q[..., MLA_NOPE:], cos[:, None, :], sin[:, None, :])], axis=-1)
    ckv_all = jnp.concatenate([ckv_ctx, _rmsnorm(dkv, kv_norm)], axis=1)
    kr_all = jnp.concatenate([kr_ctx, _rope(kr, cos, sin)], axis=1)
    k_nope, v = _mla_expand(ckv_all, w_ukv)
    y_mla = _mla_attention(q, k_nope, kr_all, v)
    out = _merge((y_na, y_sgu, y_mla), (na_z, sz, mz), mg, w_branch, w_out)
    return x + gate * out


def setup_inputs(seed: int = 0) -> dict:
    key = jax.random.key(seed)
    ks = jax.random.split(key, 24)
    f32 = jnp.float32

    def nrm(k, shape, s):
        return jax.random.normal(k, shape, f32) * s

    return {
        'x_prompt': nrm(ks[0], (BATCH, SEQ, D_MODEL), 1.0),
        'x_sample': nrm(ks[1], (DEC_BATCH, DEC_SEQ, D_MODEL), 1.0),
        'cache_na_k': nrm(ks[2], (DEC_BATCH, DEPTH, PAST_LEN, NA_HEADS, NA_HEAD_DIM), 1.0),
        'cache_na_v': nrm(ks[3], (DEC_BATCH, DEPTH, PAST_LEN, NA_HEADS, NA_HEAD_DIM), 1.0),
        'cache_mla_ckv': nrm(ks[4], (DEC_BATCH, DEPTH, PAST_LEN, MLA_KV_LORA), 1.0),
        'cache_mla_krope': nrm(ks[5], (DEC_BATCH, DEPTH, PAST_LEN, MLA_ROPE), 1.0),
        'c': nrm(ks[6], (DEC_BATCH, D_MODEL), 1.0),
        'c_ctx': nrm(ks[7], (D_MODEL,), 1.0),
        'norm_g': 1.0 + nrm(ks[8], (DEPTH, D_MODEL), 0.02),
        'w_mod': nrm(ks[9], (DEPTH, D_MODEL, 3 * D_MODEL), D_MODEL ** -0.5),
        'b_mod': nrm(ks[10], (DEPTH, 3 * D_MODEL), 0.02),
        'w_in': nrm(ks[11], (DEPTH, D_MODEL, IN_COLS), D_MODEL ** -0.5),
        'na_rpb': nrm(ks[12], (DEPTH, NA_HEADS, 2 * NA_WIN_H_MAX - 1, 2 * NA_WIN_W - 1), 0.1),
        'sgu_w': nrm(ks[13], (DEPTH, SGU_GROUPS, SGU_CHUNK, SGU_CHUNK), SGU_CHUNK ** -0.5),
        'sgu_b': nrm(ks[14], (DEPTH, SGU_GROUPS, SGU_CHUNK), 0.02),
        'mla_q_norm': 1.0 + nrm(ks[15], (DEPTH, MLA_Q_LORA), 0.02),
        'mla_w_uq': nrm(ks[16], (DEPTH, MLA_Q_LORA, MLA_HEADS * (MLA_NOPE + MLA_ROPE)), MLA_Q_LORA ** -0.5),
        'mla_kv_norm': 1.0 + nrm(ks[17], (DEPTH, MLA_KV_LORA), 0.02),
        'mla_w_ukv': nrm(ks[18], (DEPTH, MLA_KV_LORA, MLA_HEADS * (MLA_NOPE + MLA_V)), MLA_KV_LORA ** -0.5),
        'w_branch': nrm(ks[19], (DEPTH, N_BRANCH, BRANCH_WIDTH, D_MODEL), BRANCH_WIDTH ** -0.5),
        'w_out': nrm(ks[20], (DEPTH, D_MODEL, D_MODEL), D_MODEL ** -0.5),
        'final_norm_g': 1.0 + nrm(ks[21], (D_MODEL,), 0.02),
    }


def reference(x_prompt, x_sample, cache_na_k, cache_na_v, cache_mla_ckv, cache_mla_krope, c, c_ctx,
              norm_g, w_mod, b_mod, w_in, na_rpb, sgu_w, sgu_b, mla_q_norm, mla_w_uq, mla_kv_norm,
              mla_w_ukv, w_branch, w_out, final_norm_g):
    xp = x_prompt
    xs = x_sample
    cos, sin = _axial_rope(x_sample.shape[1])
    ks_, vs_, ckvs_, krs_ = [], [], [], []
    for l in range(DEPTH):
        lp = (norm_g[l], w_mod[l], b_mod[l], w_in[l], na_rpb[l], sgu_w[l], sgu_b[l], mla_q_norm[l],
              mla_w_uq[l], mla_kv_norm[l], mla_w_ukv[l], w_branch[l], w_out[l])
        xp, k_c, v_c, ckv_c, kr_c = _context_layer(xp, c_ctx, lp)
        ks_.append(k_c)
        vs_.append(v_c)
        ckvs_.append(ckv_c)
        krs_.append(kr_c)
        xs = _latent_layer(xs, c, cache_na_k[:, l], cache_na_v[:, l], cache_mla_ckv[:, l],
                           cache_mla_krope[:, l], cos, sin, lp)
    y_prompt = _rmsnorm(xp, final_norm_g)
    y_sample = _rmsnorm(xs, final_norm_g)
    state_na_k = jnp.stack(ks_, axis=1)
    state_na_v = jnp.stack(vs_, axis=1)
    state_mla_ckv = jnp.stack(ckvs_, axis=1)
    state_mla_krope = jnp.stack(krs_, axis=1)
    return (y_prompt, y_sample, state_na_k, state_na_v, state_mla_ckv, state_mla_krope)
```

```python
import numpy as np
from contextlib import ExitStack
import concourse.bass as bass
import concourse.mybir as mybir
from concourse.bass_utils import run_bass_kernel_spmd

F32 = mybir.dt.float32
BF16 = mybir.dt.bfloat16
AF = mybir.ActivationFunctionType
ALU = mybir.AluOpType

ENGS = ("pe", "act", "dve", "pool", "sp")
EPS = 1e-6
NCORES = 8
RING = 4
ARENA_BYTES = 90112


class Op:
    __slots__ = ("eng", "fn", "deps", "signals", "sigval", "dsem", "dval")

    def __init__(self, eng, fn):
        self.eng = eng
        self.fn = fn
        self.deps = set()
        self.signals = False
        self.sigval = 0
        self.dsem = None
        self.dval = 0


class DSem:
    def __init__(self, sem):
        self.sem = sem
        self.count = 0
        self.last = None


class Reg:
    __slots__ = ("space", "lo", "hi", "name", "lane", "last_w", "readers", "ov")


def _lanes_disjoint(a, b):
    for x, y in zip(a, b):
        if x is not None and y is not None and x != y:
            return True
    return False


class Prog:
    def __init__(self):
        self.ops = []
        self.eng_ops = {e: [] for e in ENGS}
        self.spaces = {}
        self.regs = {}

    def reg(self, space, lo, hi, name, lane=()):
        key = (name, lane)
        r = self.regs.get(key)
        if r is not None:
            return r
        r = Reg()
        r.space, r.lo, r.hi, r.name, r.lane = space, lo, hi, name, lane
        r.last_w = None
        r.readers = {}
        r.ov = [r]
        lst = self.spaces.setdefault(space, [])
        for q in lst:
            if q.lo < hi and lo < q.hi:
                if q.name == name and _lanes_disjoint(q.lane, lane):
                    continue
                q.ov.append(r)
                r.ov.append(q)
        lst.append(r)
        self.regs[key] = r
        return r

    def _track(self, o, reads, writes):
        deps = o.deps
        for r in reads:
            for q in r.ov:
                if q.last_w is not None:
                    deps.add(q.last_w)
        for r in writes:
            for q in r.ov:
                if q.last_w is not None:
                    deps.add(q.last_w)
                for k, x in q.readers.items():
                    if k == "dma":
                        deps.update(x)
                    else:
                        deps.add(x)
        for r in reads:
            if o.dsem is not None:
                r.readers.setdefault("dma", []).append(o)
            else:
                r.readers[o.eng] = o
        for r in writes:
            r.last_w = o
            r.readers = {}
        deps.discard(o)
        self.ops.append(o)
        self.eng_ops[o.eng].append(o)
        return o

    def op(self, eng, fn, reads=(), writes=()):
        return self._track(Op(eng, fn), reads, writes)

    def dma(self, eng, dsem, fn, reads=(), writes=()):
        o = Op(eng, fn)
        o.dsem = dsem
        dsem.count += 16
        o.dval = dsem.count
        if dsem.last is not None:
            o.deps.add(dsem.last)
        dsem.last = o
        return self._track(o, reads, writes)

    def emit(self, block, sems):
        for o in self.ops:
            for d in o.deps:
                if d.dsem is None:
                    if d.eng == "pe" and o.eng == "pe" and o.dsem is None:
                        continue
                    d.signals = True
        for e in ENGS:
            c = 0
            for o in self.eng_ops[e]:
                if o.signals:
                    c += 1
                    o.sigval = c
        blk = {"pe": block.tensor, "act": block.scalar, "dve": block.vector,
               "pool": block.gpsimd, "sp": block.sync}
        stats = {}
        for e in ENGS:
            ops = self.eng_ops[e]
            if not ops:
                continue
            nw = [0]

            def body(h, ops=ops, e=e, nw=nw):
                waited = {}
                for o in ops:
                    ws = {}
                    for d in o.deps:
                        if d.dsem is not None:
                            key = ("d", id(d.dsem))
                            sem, val = d.dsem.sem, d.dval
                        else:
                            if d.eng == "pe" and e == "pe" and o.dsem is None:
                                continue
                            key = ("e", d.eng)
                            sem, val = sems[d.eng], d.sigval
                        if waited.get(key, 0) >= val:
                            continue
                        if key not in ws or ws[key][1] < val:
                            ws[key] = (sem, val)
                    for key, (sem, val) in ws.items():
                        h.wait_ge(sem, val)
                        waited[key] = val
                        nw[0] += 1
                    ins = o.fn(h)
                    if o.dsem is not None:
                        ins.then_inc(o.dsem.sem, 16)
                    elif o.signals:
                        ins.then_inc(sems[e], 1)

            blk[e](body)
            stats[e] = (len(ops), nw[0])
        return stats


class Buf:
    def __init__(self, P, name, ap, space, lo, hi, nlane=0):
        self.P, self.name, self.ap, self.space, self.lo, self.hi, self.nlane = P, name, ap, space, lo, hi, nlane

    def r(self, *lane):
        lane = tuple(lane) + (None,) * (self.nlane - len(lane))
        return self.P.reg(self.space, self.lo, self.hi, self.name, lane)


def _na_row_lo(r):
    return min(max(r - 4, 0), 8)


def _na_tiles(t):
    us = []
    for u in range(7, -1, -1):
        quads = []
        anyv = False
        for a in range(2):
            for b in range(2):
                kr, r = 2 * u + a, 2 * t + b
                lo = _na_row_lo(r)
                v = lo <= kr <= lo + 7
                anyv = anyv or v
                if not v:
                    quads.append((a, b))
        if anyv:
            us.append((u, quads))
    return us


C_Q, C_K, C_V, C_Z = 0, 512, 1024, 1536
C_SU, C_SV, C_SZ = 2048, 2560, 3072
C_DQ, C_DKV, C_KR, C_MZ, C_MG = 3584, 3968, 4224, 4256, 4768


SUB = 99
SUB2 = 99
VENG = 'dve'


def build_program(stage=99):
    nc = bass.Bass("TRN2", target_bir_lowering=False)

    def din(name, shape):
        return nc.dram_tensor(name, list(shape), F32, kind="ExternalInput")

    def dout(name, shape):
        return nc.dram_tensor(name, list(shape), F32, kind="ExternalOutput")

    xin = [din("xp", (1024, 1024)).ap(), din("xs", (1024, 1024)).ap()]
    cnk = din("cnk", (2, 256, 512)).ap()
    cnv = din("cnv", (2, 256, 512)).ap()
    cckv = din("cckv", (2, 256, 256)).ap()
    ckr = din("ckr", (2, 256, 32)).ap()
    c2 = din("c2", (2, 1024))
    norm_g = din("norm_g", (2, 1024))
    w_mod = din("w_mod", (2, 1024, 3072)).ap()
    b_mod = din("b_mod", (2, 3072))
    w_in = din("w_in", (2, 1024, 7840)).ap()
    rpbpad = din("rpbpad", (2, 8, 15, 128))
    sgu_w = din("sgu_w", (2, 4, 128, 128)).ap()
    sgu_b = din("sgu_b", (2, 512))
    q_norm = din("q_norm", (2, 384))
    w_uq = din("w_uq", (2, 384, 768)).ap()
    kv_norm = din("kv_norm", (2, 256))
    w_ukv = din("w_ukv", (2, 256, 1024)).ap()
    w_branch = din("w_branch", (2, 3, 512, 1024)).ap()
    w_out = din("w_out", (2, 1024, 1024)).ap()
    fng = din("fng", (1024,))
    ident_d = din("ident", (128, 128)).ap()
    rope_d = din("rope", (2, 32, 1024)).ap()
    cmask_d = din("cmask", (128, 64)).ap()

    yout = [dout("yp", (1024, 1024)).ap(), dout("ys", (1024, 1024)).ap()]
    sk_o = dout("sk", (4, 2, 256, 512)).ap()
    sv_o = dout("sv", (4, 2, 256, 512)).ap()
    sckv_o = dout("sckv", (4, 2, 256, 256)).ap()
    skr_o = dout("skr", (4, 2, 256, 32)).ap()

    dbg = None
    if stage < 99:
        dbg = {"g": dout("dbg_g", (3, 128, 4096)).ap(), "mt": dout("dbg_mt", (128, 8192)).ap(),
               "ht": dout("dbg_ht", (128, 8192)).ap(), "mod": dout("dbg_mod", (128, 96)).ap()}
    P = Prog()
    with ExitStack() as es:
        es.enter_context(nc.allow_non_contiguous_dma(reason="small strided parameter loads"))

        def sbt(name, shape, dt):
            return es.enter_context(nc.sbuf_tensor(name, list(shape), dt))

        def pst(name, shape, dt):
            return es.enter_context(nc.psum_tensor(name, list(shape), dt))

        def pbuf(name, shape, dt, nlane=0):
            t = sbt(name, shape, dt)
            return Buf(P, name, t[:], name, 0, 1, nlane)

        XT = pbuf("XT", (128, 8, 1024), F32, 2)
        HT = pbuf("HT", (128, 8, 1024), BF16, 2)
        GT = [pbuf(f"G{k}", (128, 4, 1024), BF16, 1) for k in range(3)]
        ring = [pbuf(f"ring{i}", (128, 4096), BF16) for i in range(RING)]
        IDT = pbuf("ident_sb", (128, 128), F32)
        ONES = pbuf("ones_sb", (128, 128), F32)
        CMASK = pbuf("cmask_sb", (128, 64), F32)
        CT = pbuf("condT", (128, 2, 8), F32)
        CTB = pbuf("condTb", (128, 2, 8), BF16)
        MODR = pbuf("modr", (128, 2, 24, 2), F32)
        MODA = pbuf("moda", (128, 2, 8, 2), F32)
        BMODT = pbuf("bmodT", (128, 2, 24), F32)
        GNT = pbuf("gnT", (128, 2, 8), F32)
        FNGT = pbuf("fngT", (128, 8), F32)
        QNT = pbuf("qnT", (128, 2, 3), F32)
        KVNT = pbuf("kvnT", (128, 2, 2), F32)
        WST = pbuf("wsT", (128, 2, 4, 128), BF16)
        BSB = pbuf("bsb", (128, 2, 512), F32)
        KVNB = pbuf("kvnb", (128, 2, 256), F32)
        WKRP = pbuf("wkrp", (128, 8, 32), BF16)
        WUQP = pbuf("wuqp", (128, 3, 256), BF16)
        SMALL = pbuf("small", (128, 64), F32, 1)
        HMASK = pbuf("hmask", (128, 16), F32)
        ONES512 = pbuf("ones512", (128, 512), F32)

        arena_t = sbt("arena", (128, ARENA_BYTES // 4), F32)

        def aview(name, off, shape, dt, nlane=0):
            esz = 4 if dt == F32 else 2
            n = 1
            for s in shape[1:]:
                n *= s
            nbytes = n * esz
            assert off % 4 == 0 and nbytes % 4 == 0 and off + nbytes <= ARENA_BYTES, (name, off, nbytes)
            ap = arena_t[:, off // 4:(off + nbytes) // 4]
            if dt != F32:
                ap = ap.bitcast(dt)
            if len(shape) == 3:
                ap = ap.rearrange("p (a b) -> p a b", a=shape[1])
            elif len(shape) == 4:
                ap = ap.rearrange("p (a b c) -> p a b c", a=shape[1], b=shape[2])
            return Buf(P, name, ap, "arena", off, off + nbytes, nlane)

        SQ = [aview(f"SQ{i}", i * 2048, (128, 512), F32) for i in range(2)]
        RSTD = aview("RSTD", 4096, (128, 512), F32)
        TMP = [aview(f"TMP{i}", 6144 + i * 2048, (128, 512), F32) for i in range(2)]
        B0 = 10240
        SQN = [aview(f"SQN{i}", B0 + i * 2048, (128, 512), F32) for i in range(8)]
        RS2 = [RSTD, aview("RSTD2", 0, (128, 512), F32)]
        XS = [aview(f"XS{i}", B0 + i * 4096, (128, 1024), F32) for i in range(2)]
        XSL = XS + [aview(f"XS{i}", B0 + i * 4096, (128, 1024), F32) for i in range(2, 4)]
        YT = aview("YT", B0 + 8192, (128, 8, 512), F32, 1)
        XSP = [aview(f"XSP{i}", B0 + 24576 + i * 4096, (128, 1024), F32) for i in range(3)]
        QA = aview("QA", B0, (128, 4, 1024), BF16, 2)
        QA1 = aview("QA1", B0 + 65632, (128, 4, 1024), BF16, 2)
        KA = aview("KA", B0 + 8192, (128, 4, 1280), BF16, 2)
        VA1 = aview("VA1", B0 + 18432, (128, 10, 8, 66), BF16, 1)
        PTA = [aview(f"PTA{i}", B0 + 28992 + i * 1792, (128, 896), BF16) for i in range(2)] + [aview("PTA2", B0 + 63840, (128, 896), BF16)]
        PTA4 = PTA + [aview("PTA3", B0 + 73824, (128, 896), BF16)]
        YA = [aview(f"YA{i}", B0 + 32576 + i * 1024, (128, 256), F32) for i in range(2)]
        RDA = [aview(f"RDA{i}", B0 + 34624 + i * 16, (128, 4), F32) for i in range(2)]
        OSTA = [aview(f"OSTA{i}", B0 + 34656 + i * 2048, (128, 512), F32) for i in range(2)]
        CSTA = aview("CSTA", B0 + 34656, (128, 2, 512), F32)
        ERAW = [aview(f"ERAW{i}", B0 + 38752 + i * 3584, (128, 14, 64), F32) for i in range(2)]
        EEXP = aview("EEXP", B0 + 45920, (128, 14, 64), F32)
        ETAB = aview("ETAB", B0 + 49504, (128, 8, 14, 64), BF16, 1)
        ETAB2 = aview("ETAB2", B0 + 38752, (128, 8, 10, 64), BF16)
        SU = aview("SU", B0, (128, 4, 1024), BF16, 1)
        VB = aview("VB", B0 + 8192, (128, 8, 512), BF16, 1)
        GV = [aview(f"GV{i}", B0 + 16384 + i * 2048, (128, 512), F32) for i in range(3)]
        DQ = aview("DQ", B0, (128, 3, 512), F32, 1)
        DKV = aview("DKV", B0 + 6144, (128, 2, 512), F32, 1)
        DQN = aview("DQN", B0 + 10240, (128, 3, 1024), BF16, 1)
        CKVT = aview("CKVT", B0 + 16384, (128, 2, 1280), BF16, 1)
        KRT = aview("KRT", B0 + 21504, (128, 1280), BF16, 1)
        QC = aview("QC", B0 + 24064, (128, 4, 1024), BF16, 2)
        KC = aview("KC", B0 + 32256, (128, 4, 1280), BF16, 2)
        V1C = aview("V1C", B0 + 42496, (128, 10, 8, 66), BF16, 1)
        PTC = [aview(f"PTC{i}", B0 + 53056 + i * 1792, (128, 896), BF16) for i in range(2)] + [aview("PTC2", B0 + 72032, (128, 896), BF16)]
        PTC4 = PTC + [aview("PTC3", B0 + 73824, (128, 896), BF16)]
        YC = [aview(f"YC{i}", B0 + 56640 + i * 1024, (128, 256), F32) for i in range(2)]
        RDC = [aview(f"RDC{i}", B0 + 58688 + i * 16, (128, 4), F32) for i in range(2)]
        OSTC = [aview(f"OSTC{i}", B0 + 58720 + i * 1152, (128, 288), F32) for i in range(2)]
        CSTC = aview("CSTC", B0 + 61024, (128, 2, 256), F32)
        CSTK = aview("CSTK", B0 + 63072, (128, 2, 96), F32)
        COS = aview("COS", B0 + 63840, (128, 1024), F32)
        SIN = aview("SIN", B0 + 67936, (128, 1024), F32)
        ACC = aview("ACC", B0, (128, 4, 1024), F32, 2)
        MT = aview("MT", B0 + 16384, (128, 8, 1024), BF16, 2)
        SG = [aview(f"SG{i}", B0 + 32768 + i * 2048, (128, 512), F32) for i in range(2)]

        ps2 = [pst(f"ps{i}", (128, 1024), F32) for i in range(4)]

        def bank(b):
            return Buf(P, f"bank{b}", ps2[b // 2][:, (b % 2) * 512:(b % 2 + 1) * 512], "psum", b * 2048, (b + 1) * 2048)

        BANK = [bank(b) for b in range(8)]
        SBUF2 = [Buf(P, f"sbank{i}", ps2[1 + i][:], "psum", (2 + 2 * i) * 2048, (4 + 2 * i) * 2048) for i in range(3)]
        XBANK2 = Buf(P, "xbank2", ps2[0][:], "psum", 0, 4096)

        sems = {e: es.enter_context(nc.semaphore("s_" + e)) for e in ENGS}
        _nds = [0]

        def new_dsem():
            _nds[0] += 1
            return DSem(es.enter_context(nc.semaphore(f"d{_nds[0]}")))

        block = es.enter_context(nc.Block())

        def mm(out, lhsT, rhs, start, stop, reads, writes):
            P.op("pe", lambda h: h.matmul(out, lhsT=lhsT, rhs=rhs, start=start, stop=stop), reads, writes)

        def tr(out, in_, reads, writes):
            P.op("pe", lambda h: h.transpose(out, in_, IDT.ap), list(reads) + [IDT.r()], writes)

        def act(out, in_, func, reads, writes, scale=None, bias=None):
            kw = {}
            if scale is not None:
                kw["scale"] = scale
            if bias is not None:
                kw["bias"] = bias
            P.op("act", lambda h: h.activation(out=out, in_=in_, func=func, **kw), reads, writes)

        def tt(out, in0, in1, op, reads, writes, eng="dve"):
            P.op(eng, lambda h: h.tensor_tensor(out=out, in0=in0, in1=in1, op=op), reads, writes)

        def ts(out, in0, s1, s2, op0, op1, reads, writes):
            P.op("dve", lambda h: h.tensor_scalar(out=out, in0=in0, scalar1=s1, scalar2=s2, op0=op0, op1=op1), reads, writes)

        def stt(out, in0, scalar, in1, op0, op1, reads, writes):
            P.op("dve", lambda h: h.scalar_tensor_tensor(out=out, in0=in0, scalar=scalar, in1=in1, op0=op0, op1=op1), reads, writes)

        def cp(eng, out, in_, reads, writes):
            if eng == "act":
                P.op("act", lambda h: h.activation(out=out, in_=in_, func=AF.Copy), reads, writes)
            else:
                P.op("dve", lambda h: h.tensor_copy(out=out, in_=in_), reads, writes)

        def dma(eng, ds, out, in_, reads, writes):
            return P.dma(eng, ds, lambda h: h.dma_start(out=out, in_=in_), reads, writes)

        _cpi = [0]

        def alt_eng():
            _cpi[0] += 1
            return "act" if _cpi[0] % 2 else "dve"

        _pb = {}

        def next_bank(lst=(0, 1, 2, 3)):
            _pb[lst] = _pb.get(lst, -1) + 1
            return BANK[lst[_pb[lst] % len(lst)]]

        out_dmas = []

        items = []

        def win_src(l, c0, n):
            return w_in[l].rearrange("(kc p) n -> p kc n", p=128)[:, :, c0:c0 + n]

        def slab_dst(nk, n):
            return lambda rb: rb.ap[:, 0:nk * n].rearrange("p (k n) -> p k n", k=nk)

        def mod_items(l, js=range(6)):
            for j in js:
                items.append((("mod", l, j), [(slab_dst(8, 512),
                                               w_mod[l].rearrange("(kc p) n -> p kc n", p=128)[:, :, j * 512:(j + 1) * 512])]))
        unit_order = [(0, 0), (0, 1), (1, 0), (1, 1)][:min(stage, 4)]
        defer_gate0 = len(unit_order) >= 1
        mod_items(0, range(4) if defer_gate0 else range(6))
        MOD1_AFTER = {"q": 0, "k": 1, "v": 2, "z": 3, "su": 4, "sv": 5}
        interleave_mod1 = len(unit_order) >= 2
        if not interleave_mod1:
            mod_items(1)
        for (u, l) in unit_order:
            for nm, c0 in (("q", C_Q), ("k", C_K), ("v", C_V), ("z", C_Z), ("su", C_SU), ("sv", C_SV), ("sz", C_SZ)):
                items.append(((nm, u, l), [(slab_dst(8, 512), win_src(l, c0, 512))]))
                if interleave_mod1 and (u, l) == (0, 0) and nm in MOD1_AFTER:
                    j = MOD1_AFTER[nm]
                    items.append((("mod", 1, j), [(slab_dst(8, 512),
                                                   w_mod[1].rearrange("(kc p) n -> p kc n", p=128)[:, :, j * 512:(j + 1) * 512])]))
            items.append((("dq", u, l), [(slab_dst(8, 384), win_src(l, C_DQ, 384))]))
            items.append((("dkvkr", u, l), [(slab_dst(8, 288), win_src(l, C_DKV, 288))]))
            items.append((("wuq", u, l), [(slab_dst(3, 768), w_uq[l].rearrange("(kc p) n -> p kc n", p=128))]))
            items.append((("wukv", u, l), [(slab_dst(2, 1024), w_ukv[l].rearrange("(kc p) n -> p kc n", p=128))]))
            items.append((("mz", u, l), [(slab_dst(8, 512), win_src(l, C_MZ, 512))]))
            for half in range(2):
                for k in range(3):
                    items.append((("mg", u, l, k, half), [(slab_dst(8, 512), win_src(l, C_MG + k * 1024 + half * 512, 512))]))
                    items.append((("wb", u, l, k, half), [(slab_dst(4, 512),
                                                           w_branch[l, k].rearrange("(kc p) n -> p kc n", p=128)[:, :, half * 512:(half + 1) * 512])]))
            if defer_gate0 and (u, l) == (0, 0):
                mod_items(0, range(4, 6))
            for half in range(2):
                items.append((("wo", u, l, half), [(slab_dst(8, 512),
                                                    w_out[l].rearrange("(kc p) n -> p kc n", p=128)[:, :, half * 512:(half + 1) * 512])]))
        ring_ds = [new_dsem() for _ in range(RING)]
        wstate = {"issued": 0, "next": 0}
        wdone = [False] * len(items)

        def _pump():
            while wstate["issued"] < len(items):
                i = wstate["issued"]
                if i >= RING and not wdone[i - RING]:
                    break
                rb = ring[i % RING]
                for dstf, src in items[i][1]:
                    dma("pool", ring_ds[i % RING], dstf(rb), src, [], [rb.r()])
                wstate["issued"] = i + 1

        class _Item:
            def __init__(self, j):
                self.j = j
                self.rb = ring[j % RING]
                self.sl = items[j][1][0][0](self.rb)

            def release(self):
                wdone[self.j] = True
                _pump()
                key = items[self.j][0]
                if interleave_mod1 and len(key) == 3 and key[1:] == (0, 0) and key[0] in MOD1_AFTER:
                    mod_slab(1, MOD1_AFTER[key[0]])
                    if key[0] == "sv":
                        mod_finish(1)

        def acquire(key):
            j = wstate["next"]
            while items[j][0] != key and SUB < 99:
                wdone[j] = True
                j += 1
            assert items[j][0] == key, (items[j][0], key)
            wstate["next"] = j + 1
            _pump()
            assert wstate["issued"] > j, (j, key)
            return _Item(j)

        ds_c = [new_dsem() for _ in range(6)]
        dma("sp", ds_c[0], IDT.ap, ident_d, [], [IDT.r()])
        dma("sp", ds_c[1], CMASK.ap, cmask_d, [], [CMASK.r()])
        P.op("dve", lambda h: h.memset(ONES.ap, 1.0), [], [ONES.r()])
        P.op("dve", lambda h: h.memset(ONES512.ap, 1.0), [], [ONES512.r()])
        P.op("dve", lambda h: h.memset(HMASK.ap[0:64, 0:8], 1.0), [], [HMASK.r()])
        P.op("dve", lambda h: h.memset(HMASK.ap[64:128, 0:8], 0.0), [], [HMASK.r()])
        P.op("dve", lambda h: h.memset(HMASK.ap[0:64, 8:16], 0.0), [], [HMASK.r()])
        P.op("dve", lambda h: h.memset(HMASK.ap[64:128, 8:16], 1.0), [], [HMASK.r()])
        for j in range(2):
            dma("act", ds_c[2], CT.ap[:, j, :], bass.AP(tensor=c2, offset=j * 1024, ap=[[1, 128], [128, 8]]), [], [CT.r()])
        for l in range(2):
            dma("act", ds_c[3], BMODT.ap[:, l, :], bass.AP(tensor=b_mod, offset=l * 3072, ap=[[1, 128], [128, 24]]), [], [BMODT.r()])
            dma("act", ds_c[3], GNT.ap[:, l, :], bass.AP(tensor=norm_g, offset=l * 1024, ap=[[1, 128], [128, 8]]), [], [GNT.r()])
            dma("act", ds_c[4], QNT.ap[:, l, :], bass.AP(tensor=q_norm, offset=l * 384, ap=[[1, 128], [128, 3]]), [], [QNT.r()])
            dma("act", ds_c[4], KVNT.ap[:, l, :], bass.AP(tensor=kv_norm, offset=l * 256, ap=[[1, 128], [128, 2]]), [], [KVNT.r()])
            dma("act", ds_c[5], BSB.ap[:, l, :], bass.AP(tensor=sgu_b, offset=l * 512, ap=[[0, 128], [1, 512]]), [], [BSB.r()])
            dma("act", ds_c[5], KVNB.ap[:, l, :], bass.AP(tensor=kv_norm, offset=l * 256, ap=[[0, 128], [1, 256]]), [], [KVNB.r()])
        dma("act", ds_c[4], FNGT.ap, bass.AP(tensor=fng, offset=0, ap=[[1, 128], [128, 8]]), [], [FNGT.r()])
        act(CTB.ap, CT.ap, AF.Silu, [CT.r()], [CTB.r()])
        ds_w = new_dsem()
        for l in range(2):
            stg = XS[l]
            dma("sp", ds_w, stg.ap[:, 0:512].rearrange("p (g q) -> p g q", g=4), sgu_w[l].rearrange("g p q -> p g q"), [], [stg.r()])
            bk = next_bank()
            for g in range(4):
                tr(bk.ap[:, g * 128:(g + 1) * 128], stg.ap[:, g * 128:(g + 1) * 128], [stg.r()], [bk.r()])
            cp("dve", WST.ap[:, l].rearrange("p g q -> p (g q)"), bk.ap, [bk.r()], [WST.r()])

        def mod_slab(l, j):
            it = acquire(("mod", l, j))
            rb, sl = it.rb, it.sl
            pm = next_bank()
            for cc in range(4):
                for kc in range(8):
                    mm(pm.ap[:, cc * 2:cc * 2 + 2], sl[:, kc, cc * 128:(cc + 1) * 128], CTB.ap[:, :, kc],
                       kc == 0, kc == 7, [rb.r(), CTB.r()], [pm.r()])
            tt(MODR.ap[:, l, j * 4:(j + 1) * 4, :], pm.ap[:, 0:8].rearrange("p (c j) -> p c j", j=2),
               BMODT.ap[:, l, j * 4:(j + 1) * 4].unsqueeze(2).to_broadcast([128, 4, 2]), ALU.add,
               [pm.r(), BMODT.r()], [MODR.r()])
            it.release()

        def mod_finish(l):
            stt(MODA.ap[:, l], MODR.ap[:, l, 8:16, :], 1.0, GNT.ap[:, l, :].unsqueeze(2).to_broadcast([128, 8, 2]),
                ALU.add, ALU.mult, [MODR.r(), GNT.r()], [MODA.r()])

        def mod_phase(l):
            for j in range(4 if (l == 0 and defer_gate0) else 6):
                mod_slab(l, j)
            mod_finish(l)

        def modA(l, c, cond):
            return MODA.ap[:, l, c, cond:cond + 1]

        def modS(l, c, cond):
            return MODR.ap[:, l, c, cond:cond + 1]

        def modG(l, c, cond):
            return MODR.ap[:, l, 16 + c, cond:cond + 1]

        def slab(s):
            return slice(s * 512, (s + 1) * 512)

        def rms_stats(src_ap_fn, src_regs_fn, nchunks, dim, bk):
            for c in range(nchunks):
                sq = SQ[c % 2]
                act(sq.ap, src_ap_fn(c), AF.Square, src_regs_fn(c), [sq.r()])
                mm(bk.ap, ONES.ap, sq.ap, c == 0, c == nchunks - 1, [ONES.r(), sq.r()], [bk.r()])
            act(RSTD.ap, bk.ap, AF.Ln, [bk.r()], [RSTD.r()], scale=1.0 / dim, bias=EPS)
            act(RSTD.ap, RSTD.ap, AF.Exp, [RSTD.r()], [RSTD.r()], scale=-0.5)

        def norm_phase(l, cond):
            bks = [next_bank(), next_bank()]
            for s in range(2):
                for c in range(8):
                    sq = SQN[c]
                    act(sq.ap, XT.ap[:, c, slab(s)], AF.Square, [XT.r(c, s)], [sq.r()])
                    mm(bks[s].ap, ONES.ap, sq.ap, c == 0, c == 7, [ONES.r(), sq.r()], [bks[s].r()])
            for s in range(2):
                rs = RS2[s]
                act(rs.ap, bks[s].ap, AF.Ln, [bks[s].r()], [rs.r()], scale=1.0 / 1024, bias=EPS)
            for s in range(2):
                rs = RS2[s]
                act(rs.ap, rs.ap, AF.Exp, [rs.r()], [rs.r()], scale=-0.5)
            for s in range(2):
                rs = RS2[s]
                for c in range(8):
                    tm = TMP[c % 2]
                    tt(tm.ap, XT.ap[:, c, slab(s)], rs.ap, ALU.mult, [XT.r(c, s), rs.r()], [tm.r()])
                    act(HT.ap[:, c, slab(s)], tm.ap, AF.Identity, [tm.r(), MODA.r(), MODR.r()], [HT.r(s, c)],
                        scale=modA(l, c, cond), bias=modS(l, c, cond))

        def projT(rb, sl, col0, m, src, nk, evac, prow=0, n_of=None):
            for s in range(2):
                bk = next_bank()
                for kc in range(nk):
                    mm(bk.ap[prow:prow + m, :], sl[:, kc, col0:col0 + m], src.ap[:, kc, slab(s)],
                       kc == 0, kc == nk - 1, [rb.r(), src.r(s)], [bk.r()])
                evac(s, bk)

        etab_built = set()
        etab_ds = []

        def etab_head(l, h_):
            if (l, h_) in etab_built:
                return
            etab_built.add((l, h_))
            if not etab_ds:
                etab_ds.extend([new_dsem(), new_dsem()])
            er = ERAW[h_ % 2]
            for a in range(2):
                src = bass.AP(tensor=rpbpad, offset=((l * 8 + h_) * 15 + a + 13) * 128,
                              ap=[[1, 64], [-128, 14], [1, 64]])
                dma("sp", etab_ds[h_ % 2], er.ap[a * 64:(a + 1) * 64], src, [], [er.r()])
            er_rev = bass.AP(tensor=er.ap.tensor, offset=er.ap.offset + 63,
                             ap=[list(er.ap.ap[0]), list(er.ap.ap[1]), [-1, 64]])
            act(EEXP.ap, er_rev, AF.Exp, [er.r()], [EEXP.r()])
            tt(ETAB.ap[:, h_], EEXP.ap, CMASK.ap.unsqueeze(1).to_broadcast([128, 14, 64]), ALU.mult,
               [EEXP.r(), CMASK.r()], [ETAB.r(h_)])
            if h_ == 7:
                cp("dve", ETAB2.ap, ETAB.ap[:, :, 2:12, :], [ETAB.r()], [ETAB2.r()])
                P.op("dve", lambda h: h.memset(ETAB2.ap[0:64, :, 0, :], 0.0), [], [ETAB2.r()])
                P.op("dve", lambda h: h.memset(ETAB2.ap[0:64, :, 9, :], 0.0), [], [ETAB2.r()])
                P.op("dve", lambda h: h.memset(ETAB2.ap[64:128, :, 0:2, :], 0.0), [], [ETAB2.r()])

        prefetched_x = set()

        def prefetch_x(u):
            dsp = [new_dsem() for _ in range(3)]
            for t in range(3):
                dma("sp", dsp[t], XSP[t].ap, xin[u][t * 128:(t + 1) * 128, :], [], [XSP[t].r()])
                prefetched_x.add((u, t))

        def load_unit(u):
            ds_x = [new_dsem() for _ in range(4)]
            for t in range(8):
                if (u, t) in prefetched_x:
                    xs = XSP[t]
                else:
                    xs = XSL[t % 4]
                    dma("sp", ds_x[t % 4], xs.ap, xin[u][t * 128:(t + 1) * 128, :], [], [xs.r()])
                if u == 1 and (1, 0) in unit_order:
                    etab_head(0, t)
                for hb in range(2):
                    bk = BANK[(2 * t + hb) % 4]
                    for j in range(4):
                        c = hb * 4 + j
                        tr(bk.ap[:, j * 128:(j + 1) * 128], xs.ap[:, c * 128:(c + 1) * 128], [xs.r()], [bk.r()])
                    cp(alt_eng(), XT.ap[:, hb * 4:hb * 4 + 4, t * 128:(t + 1) * 128],
                       bk.ap.rearrange("p (c n) -> p c n", c=4), [bk.r()],
                       [XT.r(hb * 4 + j, t // 4) for j in range(4)])

        def final_unit(u):
            ds_y = [new_dsem(), new_dsem()]
            for s in range(2):
                bk = next_bank()
                rms_stats(lambda c: XT.ap[:, c, slab(s)], lambda c: [XT.r(c, s)], 8, 1024, bk)
                for c in range(8):
                    tm = TMP[c % 2]
                    tt(tm.ap, XT.ap[:, c, slab(s)], RSTD.ap, ALU.mult, [XT.r(c, s), RSTD.r()], [tm.r()])
                    act(YT.ap[:, c, :], tm.ap, AF.Identity, [tm.r(), FNGT.r()], [YT.r(c)], scale=FNGT.ap[:, c:c + 1])
                for tq in range(4):
                    t = s * 4 + tq
                    xs = XS[t % 2]
                    for hb in range(2):
                        bk2 = BANK[(2 * t + hb) % 4]
                        for j in range(4):
                            c = hb * 4 + j
                            tr(bk2.ap[:, j * 128:(j + 1) * 128], YT.ap[:, c, tq * 128:(tq + 1) * 128], [YT.r(c)], [bk2.r()])
                        cp(alt_eng(), xs.ap[:, hb * 512:(hb + 1) * 512], bk2.ap, [bk2.r()], [xs.r()])
                    out_dmas.append(dma("sp", ds_y[t % 2], yout[u][t * 128:(t + 1) * 128, :], xs.ap, [xs.r()], []))

        def attention(u, mixer, l, Gk, Qh, Kh, V1, PT, Y, RD, key_tiles, scale, hg, etab=None):
            steps = []
            for t in range(8):
                for hh in range(4):
                    steps.append((t, hh))
            obank = BANK[0]
            tbank = BANK[1]

            def do_S(n):
                t, hh = steps[n]
                tiles = key_tiles(t)
                sb_ = SBUF2[n % 3]
                qap, qregs = Qh(hh)
                kap, kregs = Kh(hh)
                for j, (kt, kind, extra) in enumerate(tiles):
                    mm(sb_.ap[:, j * 128:(j + 1) * 128], kap[:, kt * 128:(kt + 1) * 128], qap[:, t * 128:(t + 1) * 128],
                       True, True, kregs + qregs(t), [sb_.r()])

            def do_exp(n):
                t, hh = steps[n]
                h_ = hg * 4 + hh
                tiles = key_tiles(t)
                nt = len(tiles)
                sb_ = SBUF2[n % 3]
                pt = PT[n % 3]
                act(pt.ap[:, 0:nt * 128], sb_.ap[:, 0:nt * 128], AF.Exp, [sb_.r()], [pt.r()], scale=scale)
                mj = [j for j, x in enumerate(tiles) if x[1] == "m"]
                if mj:
                    j0 = mj[0]
                    nu = len(mj)
                    u0 = tiles[j0][2][0]
                    i0 = 7 - 2 * (u0 - t) - 1
                    interior = any(tiles[j][2][1] for j in mj)
                    if interior:
                        assert nu == 5 and i0 == 2, (t, nu, i0)
                        tab, tab_r = ETAB2.ap[:, h_, 0:10, :], ETAB2.r()
                    else:
                        tab, tab_r = etab.ap[:, h_, i0:i0 + 2 * nu, :], etab.r(h_)
                    tt(pt.ap[:, j0 * 128:(j0 + nu) * 128], pt.ap[:, j0 * 128:(j0 + nu) * 128],
                       tab.rearrange("p i c -> p (i c)"), ALU.mult, [pt.r(), tab_r], [pt.r()])

            def do_PV(n):
                t, hh = steps[n]
                h_ = hg * 4 + hh
                tiles = key_tiles(t)
                nt = len(tiles)
                pt = PT[n % 3]
                for j, (kt, kind, extra) in enumerate(tiles):
                    mm(obank.ap[:, hh * 65:(hh + 1) * 65], pt.ap[:, j * 128:(j + 1) * 128], V1.ap[:, kt, h_, 0:65],
                       j == 0, j == nt - 1, [pt.r(), V1.r(kt)], [obank.r()])
                if hh == 3:
                    y = Y[t % 2]
                    rd = RD[t % 2]
                    o3 = obank.ap[:, 0:260].rearrange("p (h d) -> p h d", h=4)
                    P.op("dve", lambda h: h.reciprocal(out=rd.ap, in_=o3[:, :, 64]), [obank.r()], [rd.r()])
                    tt(y.ap.rearrange("p (h d) -> p h d", h=4), o3[:, :, 0:64], rd.ap.unsqueeze(2).to_broadcast([128, 4, 64]),
                       ALU.mult, [obank.r(), rd.r()], [y.r()])
                    for j in range(2):
                        tr(tbank.ap[:, j * 128:(j + 1) * 128], y.ap[:, j * 128:(j + 1) * 128], [y.r()], [tbank.r()])
                    for j in range(2):
                        cp("act", Gk.ap[:, hg * 2 + j, t * 128:(t + 1) * 128], tbank.ap[:, j * 128:(j + 1) * 128],
                           [tbank.r()], [Gk.r(t // 4)])

            do_S(0)
            do_exp(0)
            do_S(1)
            do_exp(1)
            for n in range(len(steps)):
                if n + 2 < len(steps):
                    do_S(n + 2)
                    do_exp(n + 2)
                do_PV(n)

        def attention2(Gk, Qh, Kh, V1, PT4, Y, RD, groups_fn, scale, hg, etab=None):
            steps = [(t, hh, gi) for t in range(8) for hh in range(4) for gi in range(len(groups_fn(t)))]
            R = 3

            def do_S(n):
                t, hh, gi = steps[n]
                grp = groups_fn(t)[gi]
                sb_ = BANK[4 + n % 4]
                pt = PT4[n % 4]
                qap, qregs = Qh(hh)
                kap, kregs = Kh(hh)
                for j, ent in enumerate(grp):
                    kt = ent[0] if isinstance(ent, tuple) else ent
                    mm(sb_.ap[:, j * 128:(j + 1) * 128], kap[:, kt * 128:(kt + 1) * 128], qap[:, t * 128:(t + 1) * 128],
                       True, True, kregs + qregs(t), [sb_.r()])
                w = len(grp) * 128
                act(pt.ap[:, 0:w], sb_.ap[:, 0:w], AF.Exp, [sb_.r()], [pt.r()], scale=scale)
                mj = [j for j, x in enumerate(grp) if isinstance(x, tuple) and x[1] == "m"]
                if mj:
                    h_ = hg * 4 + hh
                    j0 = mj[0]
                    nu = len(mj)
                    u0 = grp[j0][2][0]
                    i0 = 7 - 2 * (u0 - t) - 1
                    tt(pt.ap[:, j0 * 128:(j0 + nu) * 128], pt.ap[:, j0 * 128:(j0 + nu) * 128],
                       etab.ap[:, h_, i0:i0 + 2 * nu, :].rearrange("p i c -> p (i c)"), ALU.mult,
                       [pt.r(), etab.r(h_)], [pt.r()])
                    for j in mj:
                        for (a, b) in grp[j][2][1]:
                            P.op("dve", lambda h, j=j, a=a, b=b, pt=pt: h.memset(pt.ap[a * 64:(a + 1) * 64, j * 128 + b * 64:j * 128 + (b + 1) * 64], 0.0),
                                 [], [pt.r()])

            def do_PV(n):
                t, hh, gi = steps[n]
                grps = groups_fn(t)
                grp = grps[gi]
                h_ = hg * 4 + hh
                ob = BANK[t % 2]
                pt = PT4[n % 4]
                for j, ent in enumerate(grp):
                    kt = ent[0] if isinstance(ent, tuple) else ent
                    mm(ob.ap[:, hh * 65:(hh + 1) * 65], pt.ap[:, j * 128:(j + 1) * 128], V1.ap[:, kt, h_, 0:65],
                       gi == 0 and j == 0, gi == len(grps) - 1 and j == len(grp) - 1, [pt.r(), V1.r(kt)], [ob.r()])

            def fin(t):
                ob = BANK[t % 2]
                tb = BANK[2 + t % 2]
                y = Y[t % 2]
                rd = RD[t % 2]
                o3 = ob.ap[:, 0:260].rearrange("p (h d) -> p h d", h=4)
                P.op("dve", lambda h: h.reciprocal(out=rd.ap, in_=o3[:, :, 64]), [ob.r()], [rd.r()])
                tt(y.ap.rearrange("p (h d) -> p h d", h=4), o3[:, :, 0:64], rd.ap.unsqueeze(2).to_broadcast([128, 4, 64]),
                   ALU.mult, [ob.r(), rd.r()], [y.r()])
                for j in range(2):
                    tr(tb.ap[:, j * 128:(j + 1) * 128], y.ap[:, j * 128:(j + 1) * 128], [y.r()], [tb.r()])
                for j in range(2):
                    cp("act", Gk.ap[:, hg * 2 + j, t * 128:(t + 1) * 128], tb.ap[:, j * 128:(j + 1) * 128],
                       [tb.r()], [Gk.r(t // 4)])

            pending = []
            for n in range(min(R, len(steps))):
                do_S(n)
            for n in range(len(steps)):
                if n + R < len(steps):
                    do_S(n + R)
                do_PV(n)
                t, hh, gi = steps[n]
                if hh == 3 and gi == len(groups_fn(t)) - 1:
                    pending.append((n + 2, t))
                while pending and pending[0][0] <= n:
                    fin(pending.pop(0)[1])
            for _, t in pending:
                fin(t)

        def gate_mul(Gk, rb, sl):
            for c in range(4):
                def ev(s, bk, c=c):
                    tm = TMP[(2 * c + s) % 2]
                    act(tm.ap, bk.ap, AF.Silu, [bk.r()], [tm.r()])
                    tt(Gk.ap[:, c, slab(s)], Gk.ap[:, c, slab(s)], tm.ap, ALU.mult, [Gk.r(s), tm.r()], [Gk.r(s)])
                projT(rb, sl, c * 128, 128, HT, 8, ev)

        def run_layer(u, l):
            cond = u
            sample = (u == 1)
            koff = 256 if sample else 0
            nkt = 10 if sample else 8
            norm_phase(l, cond)
            if SUB <= 0:
                return

            G = GT[0]
            P.op("dve", lambda h: h.memset(VA1.ap[:, :, :, 64:65], 1.0), [], [VA1.r()])
            if sample:
                ds_ca = [new_dsem(), new_dsem(), new_dsem()]
                dma("sp", ds_ca[0], CSTA.ap, cnk[l].rearrange("(kt p) n -> p kt n", p=128), [], [CSTA.r()])
                for kt in range(2):
                    bk = next_bank()
                    for c in range(4):
                        tr(bk.ap[:, c * 128:(c + 1) * 128], CSTA.ap[:, kt, c * 128:(c + 1) * 128], [CSTA.r()], [bk.r()])
                    cp("dve", KA.ap[:, :, kt * 128:(kt + 1) * 128], bk.ap.rearrange("p (c n) -> p c n", c=4),
                       [bk.r()], [KA.r(None, 0)])
                dma("sp", ds_ca[0], CSTA.ap, cnv[l].rearrange("(kt p) n -> p kt n", p=128), [], [CSTA.r()])
                for kt in range(2):
                    cp("dve", VA1.ap[:, kt, :, 0:64], CSTA.ap[:, kt, :].rearrange("p (h d) -> p h d", h=8),
                       [CSTA.r()], [VA1.r(kt)])
                for h_ in range(8):
                    etab_head(l, h_)
            it = acquire(("q", u, l)); rb, sl = it.rb, it.sl
            for s in range(2):
                for c in range(4):
                    bk = next_bank()
                    for kc in range(8):
                        mm(bk.ap, sl[:, kc, c * 128:(c + 1) * 128], HT.ap[:, kc, slab(s)], kc == 0, kc == 7,
                           [rb.r(), HT.r(s, kc)], [bk.r()])
                    stt(QA.ap[:, c, slab(s)], bk.ap, HMASK.ap[:, 0:1], ONES512.ap, ALU.mult, ALU.mult,
                        [bk.r(), HMASK.r(), ONES512.r()], [QA.r(c, s)])
                    stt(QA1.ap[:, c, slab(s)], bk.ap, HMASK.ap[:, 8:9], ONES512.ap, ALU.mult, ALU.mult,
                        [bk.r(), HMASK.r(), ONES512.r()], [QA1.r(c, s)])
            it.release()
            if SUB == 1 and SUB2 <= 0:
                return
            it = acquire(("k", u, l)); rb, sl = it.rb, it.sl
            for c in range(4):
                def ev(s, bk, c=c):
                    cp(alt_eng(), KA.ap[:, c, koff + s * 512:koff + (s + 1) * 512], bk.ap, [bk.r()], [KA.r(c, 1 + s)])
                projT(rb, sl, c * 128, 128, HT, 8, ev)
            ds_o = [new_dsem() for _ in range(2)]
            if SUB == 1 and SUB2 <= 1:
                return
            if not sample:
                for t in range(8):
                    bk = next_bank()
                    for kc in range(8):
                        mm(bk.ap, HT.ap[:, kc, t * 128:(t + 1) * 128], sl[:, kc, :], kc == 0, kc == 7,
                           [rb.r(), HT.r(t // 4)], [bk.r()])
                    ost = OSTA[t % 2]
                    cp(alt_eng(), ost.ap, bk.ap, [bk.r()], [ost.r()])
                    out_dmas.append(dma("sp", ds_o[t % 2], sk_o[t // 2, l, (t % 2) * 128:(t % 2 + 1) * 128, :], ost.ap, [ost.r()], []))
            it.release()
            if SUB == 1 and SUB2 <= 2:
                return
            it = acquire(("v", u, l)); rb, sl = it.rb, it.sl
            for t in range(8):
                bk = next_bank()
                for kc in range(8):
                    mm(bk.ap, HT.ap[:, kc, t * 128:(t + 1) * 128], sl[:, kc, :], kc == 0, kc == 7,
                       [rb.r(), HT.r(t // 4)], [bk.r()])
                kt = t + (2 if sample else 0)
                cp(VENG, VA1.ap[:, kt, :, 0:64], bk.ap.rearrange("p (h d) -> p h d", h=8), [bk.r()], [VA1.r(kt)])
                if not sample:
                    ost = OSTA[t % 2]
                    cp("dve", ost.ap, bk.ap, [bk.r()], [ost.r()])
                    out_dmas.append(dma("sp", ds_o[t % 2], sv_o[t // 2, l, (t % 2) * 128:(t % 2 + 1) * 128, :], ost.ap, [ost.r()], []))

            it.release()
            if SUB <= 1:
                return
            if sample:
                def ktiles_a(t):
                    tl = [(0, "p", None), (1, "p", None)]
                    for (uu, quads) in _na_tiles(t):
                        tl.append((2 + uu, "m", (uu, quads)))
                    return tl
            else:
                def ktiles_a(t):
                    return [(2 * (t // 2), "p", None), (2 * (t // 2) + 1, "p", None)]
            for hg in range(2):
                def Qh(hh, hg=hg):
                    h_ = hg * 4 + hh
                    c = h_ // 2
                    qb = QA if h_ % 2 == 0 else QA1
                    return qb.ap[:, c, :], (lambda t, c=c, qb=qb: [qb.r(c, t // 4)])

                def Kh(hh, hg=hg):
                    h_ = hg * 4 + hh
                    c = h_ // 2
                    return KA.ap[:, c, :], [KA.r(c, None)]
                if sample:
                    def groups_a(t):
                        tl = ktiles_a(t)
                        return [tl[0:4], tl[4:]] if len(tl) > 4 else [tl]
                    attention(u, 0, l, G, Qh, Kh, VA1, PTA, YA, RDA, ktiles_a, 0.125, hg, etab=ETAB)
                else:
                    attention2(G, Qh, Kh, VA1, PTA4, YA, RDA, lambda t: [[2 * (t // 2), 2 * (t // 2) + 1]], 0.125, hg)
            it = acquire(("z", u, l)); rb, sl = it.rb, it.sl
            gate_mul(G, rb, sl)
            it.release()

            if SUB <= 2:
                return
            G = GT[1]
            ds_cc = [new_dsem() for _ in range(5)]
            if sample:
                dma("sp", ds_cc[0], COS.ap[64:96, :], rope_d[0], [], [COS.r()])
                dma("sp", ds_cc[0], SIN.ap[64:96, :], rope_d[1], [], [SIN.r()])
                dma("sp", ds_cc[1], CSTC.ap, cckv[l].rearrange("(kt p) n -> p kt n", p=128), [], [CSTC.r()])
                P.op("dve", lambda h: h.memset(CSTK.ap, 0.0), [], [CSTK.r()])
                dma("sp", ds_cc[2], CSTK.ap[:, :, 64:96], ckr[l].rearrange("(kt p) n -> p kt n", p=128), [], [CSTK.r()])
                krs = win_src(l, C_KR, 32)
                dma("pool", ds_cc[3], WKRP.ap[:, :, 0:16], krs[:, :, 16:32], [], [WKRP.r()])
                dma("pool", ds_cc[3], WKRP.ap[:, :, 16:32], krs[:, :, 0:16], [], [WKRP.r()])
                uqs = w_uq[l].rearrange("(kc p) (h d) -> p kc h d", p=128, h=8)
                wq4 = WUQP.ap.rearrange("p k (h d) -> p k h d", h=8)
                for kc in range(3):
                    dma("pool", ds_cc[4], wq4[:, kc, :, 0:16], uqs[:, kc, :, 80:96], [], [WUQP.r()])
                    dma("pool", ds_cc[4], wq4[:, kc, :, 16:32], uqs[:, kc, :, 64:80], [], [WUQP.r()])
            it = acquire(("su", u, l)); rb, sl = it.rb, it.sl
            for c in range(4):
                def ev(s, bk, c=c):
                    act(SU.ap[:, c, slab(s)], bk.ap, AF.Gelu_apprx_tanh, [bk.r()], [SU.r(s)])
                projT(rb, sl, c * 128, 128, HT, 8, ev)
            it.release()
            it = acquire(("sv", u, l)); rb, sl = it.rb, it.sl

            def sv_proj(t):
                bk = next_bank()
                for kc in range(8):
                    mm(bk.ap, HT.ap[:, kc, t * 128:(t + 1) * 128], sl[:, kc, :], kc == 0, kc == 7,
                       [rb.r(), HT.r(t // 4)], [bk.r()])
                gv = GV[t % 3]
                act(gv.ap, bk.ap, AF.Gelu_apprx_tanh, [bk.r()], [gv.r()])
                so = (t % 3) * 16
                st_ap = SMALL.ap[:, so:so + 6]
                mv_ap = SMALL.ap[:, so + 8:so + 10]
                rs_ap = SMALL.ap[:, so + 10:so + 11]
                sr = SMALL.r(t % 3)
                P.op("dve", lambda h, gv=gv, st_ap=st_ap: h.bn_stats(out=st_ap, in_=gv.ap), [gv.r()], [sr])
                P.op("dve", lambda h, st_ap=st_ap, mv_ap=mv_ap: h.bn_aggr(out=mv_ap, in_=st_ap), [sr], [sr])
                act(rs_ap, mv_ap[:, 1:2], AF.Sqrt, [sr], [sr], scale=1.0, bias=EPS)
                P.op("dve", lambda h, rs_ap=rs_ap: h.reciprocal(out=rs_ap, in_=rs_ap), [sr], [sr])
                ts(VB.ap[:, t, :], gv.ap, mv_ap[:, 0:1], rs_ap, ALU.subtract, ALU.mult, [gv.r(), sr], [VB.r(t)])

            def sv_sgu(t):
                bk2 = next_bank((4, 5))
                for g in range(4):
                    mm(bk2.ap[:, g * 128:(g + 1) * 128], VB.ap[:, t, g * 128:(g + 1) * 128], WST.ap[:, l, g, :],
                       True, True, [VB.r(t), WST.r()], [bk2.r()])
                tm = TMP[t % 2]
                tt(tm.ap, bk2.ap, BSB.ap[:, l, :], ALU.add, [bk2.r(), BSB.r()], [tm.r()])
                tt(G.ap[:, :, t * 128:(t + 1) * 128], tm.ap.rearrange("p (g n) -> p g n", g=4),
                   SU.ap[:, :, t * 128:(t + 1) * 128], ALU.mult, [tm.r(), SU.r(t // 4)], [G.r(t // 4)])

            sv_proj(0)
            sv_proj(1)
            for t in range(8):
                if t + 2 < 8:
                    sv_proj(t + 2)
                sv_sgu(t)
            it.release()
            it = acquire(("sz", u, l)); rb, sl = it.rb, it.sl
            gate_mul(G, rb, sl)
            it.release()

            if SUB <= 3:
                return
            G = GT[2]
            P.op("dve", lambda h: h.memset(V1C.ap[:, :, :, 64:65], 1.0), [], [V1C.r()])
            if sample:
                for kt in range(2):
                    bk = next_bank()
                    for c in range(2):
                        tr(bk.ap[:, c * 128:(c + 1) * 128], CSTC.ap[:, kt, c * 128:(c + 1) * 128], [CSTC.r()], [bk.r()])
                    cp("dve", CKVT.ap[:, :, kt * 128:(kt + 1) * 128], bk.ap[:, 0:256].rearrange("p (c n) -> p c n", c=2),
                       [bk.r()], [CKVT.r(0)])
                bk = next_bank()
                for kt in range(2):
                    tr(bk.ap[0:96, kt * 128:(kt + 1) * 128], CSTK.ap[:, kt, :], [CSTK.r()], [bk.r()])
                cp("act", KRT.ap[64:96, 0:256], bk.ap[64:96, 0:256], [bk.r()], [KRT.r(0)])
            it_dq = acquire(("dq", u, l))
            it = acquire(("dkvkr", u, l)); rb, sl = it.rb, it.sl

            def projA(s):
                for c in range(3):
                    bk = BANK[c]
                    for kc in range(8):
                        mm(bk.ap, it_dq.sl[:, kc, c * 128:(c + 1) * 128], HT.ap[:, kc, slab(s)], kc == 0, kc == 7,
                           [it_dq.rb.r(), HT.r(s)], [bk.r()])
                    act(SQN[c].ap, bk.ap, AF.Square, [bk.r()], [SQN[c].r()])

            def statsA(s):
                bkr = BANK[5]
                for c in range(3):
                    mm(bkr.ap, ONES.ap, SQN[c].ap, c == 0, c == 2, [ONES.r(), SQN[c].r()], [bkr.r()])
                rs = RS2[0]
                act(rs.ap, bkr.ap, AF.Ln, [bkr.r()], [rs.r()], scale=1.0 / 384, bias=EPS)
                act(rs.ap, rs.ap, AF.Exp, [rs.r()], [rs.r()], scale=-0.5)
                for c in range(3):
                    stt(DQN.ap[:, c, slab(s)], BANK[c].ap, QNT.ap[:, l, c:c + 1], rs.ap, ALU.mult, ALU.mult,
                        [BANK[c].r(), QNT.r(), rs.r()], [DQN.r(s)])

            def projB(s):
                for c in range(2):
                    bk = BANK[3 + c]
                    for kc in range(8):
                        mm(bk.ap, sl[:, kc, c * 128:(c + 1) * 128], HT.ap[:, kc, slab(s)], kc == 0, kc == 7,
                           [rb.r(), HT.r(s)], [bk.r()])
                    act(SQN[3 + c].ap, bk.ap, AF.Square, [bk.r()], [SQN[3 + c].r()])

            def statsB(s):
                bkr = BANK[5]
                for c in range(2):
                    mm(bkr.ap, ONES.ap, SQN[3 + c].ap, c == 0, c == 1, [ONES.r(), SQN[3 + c].r()], [bkr.r()])
                rs = RS2[1]
                act(rs.ap, bkr.ap, AF.Ln, [bkr.r()], [rs.r()], scale=1.0 / 256, bias=EPS)
                act(rs.ap, rs.ap, AF.Exp, [rs.r()], [rs.r()], scale=-0.5)
                for c in range(2):
                    stt(CKVT.ap[:, c, koff + s * 512:koff + (s + 1) * 512], BANK[3 + c].ap, KVNT.ap[:, l, c:c + 1], rs.ap,
                        ALU.mult, ALU.mult, [BANK[3 + c].r(), KVNT.r(), rs.r()], [CKVT.r(1 + s)])

            for s in range(2):
                projA(s)
                projB(s)
                statsA(s)
                statsB(s)
            it_dq.release()
            for s in range(2):
                bk = next_bank((6, 7))
                for kc in range(8):
                    mm(bk.ap[64:96, :], sl[:, kc, 256:288], HT.ap[:, kc, slab(s)], kc == 0, kc == 7, [rb.r(), HT.r(s)], [bk.r()])
                if not sample:
                    cp("act", KRT.ap[64:96, s * 512:(s + 1) * 512], bk.ap[64:96, :], [bk.r()], [KRT.r(1 + s)])
                else:
                    bk2 = next_bank((6, 7))
                    for kc in range(8):
                        mm(bk2.ap[64:96, :], WKRP.ap[:, kc, :], HT.ap[:, kc, slab(s)], kc == 0, kc == 7, [WKRP.r(), HT.r(s)], [bk2.r()])
                    t0, t1 = TMP[0], TMP[1]
                    tt(t0.ap[64:96, :], bk.ap[64:96, :], COS.ap[64:96, slab(s)], ALU.mult, [bk.r(), COS.r()], [t0.r()])
                    tt(t1.ap[64:96, :], bk2.ap[64:96, :], SIN.ap[64:96, slab(s)], ALU.mult, [bk2.r(), SIN.r()], [t1.r()])
                    tt(KRT.ap[64:96, koff + s * 512:koff + (s + 1) * 512], t0.ap[64:96, :], t1.ap[64:96, :], ALU.add,
                       [t0.r(), t1.r()], [KRT.r(1 + s)])
            if not sample:
                for t in range(8):
                    bk = next_bank()
                    for kc in range(8):
                        mm(bk.ap[:, 0:288], HT.ap[:, kc, t * 128:(t + 1) * 128], sl[:, kc, :], kc == 0, kc == 7,
                           [rb.r(), HT.r(t // 4)], [bk.r()])
                    so = 32 + (t % 2) * 16
                    st_ap = SMALL.ap[:, so:so + 6]
                    mv_ap = SMALL.ap[:, so + 8:so + 10]
                    rs_ap = SMALL.ap[:, so + 10:so + 11]
                    sr = SMALL.r(2 + t % 2)
                    P.op("dve", lambda h, bk=bk, st_ap=st_ap: h.bn_stats(out=st_ap, in_=bk.ap[:, 0:256]), [bk.r()], [sr])
                    P.op("dve", lambda h, st_ap=st_ap, mv_ap=mv_ap: h.bn_aggr(out=mv_ap, in_=st_ap), [sr], [sr])
                    stt(rs_ap, mv_ap[:, 0:1], mv_ap[:, 0:1], mv_ap[:, 1:2], ALU.mult, ALU.add, [sr], [sr])
                    act(rs_ap, rs_ap, AF.Sqrt, [sr], [sr], scale=1.0, bias=EPS)
                    P.op("dve", lambda h, rs_ap=rs_ap: h.reciprocal(out=rs_ap, in_=rs_ap), [sr], [sr])
                    ost = OSTC[t % 2]
                    stt(ost.ap[:, 0:256], bk.ap[:, 0:256], rs_ap, KVNB.ap[:, l, :], ALU.mult, ALU.mult,
                        [bk.r(), sr, KVNB.r()], [ost.r()])
                    cp("act", ost.ap[:, 256:288], bk.ap[:, 256:288], [bk.r()], [ost.r()])
                    rows = slice((t % 2) * 128, (t % 2 + 1) * 128)
                    out_dmas.append(dma("sp", ds_o[t % 2], sckv_o[t // 2, l, rows, :], ost.ap[:, 0:256], [ost.r()], []))
                    out_dmas.append(dma("sp", ds_o[t % 2], skr_o[t // 2, l, rows, :], ost.ap[:, 256:288], [ost.r()], []))
            it.release()
            itq = acquire(("wuq", u, l)); rbq, slq = itq.rb, itq.sl
            itk = acquire(("wukv", u, l)); rbk, slk = itk.rb, itk.sl
            vcols = slk.rearrange("p k (h e) -> p k h e", h=8)
            for kt in range(nkt):
                bk = next_bank()
                for kc in range(2):
                    mm(bk.ap, CKVT.ap[:, kc, kt * 128:(kt + 1) * 128], vcols[:, kc, :, 64:128], kc == 0, kc == 1,
                       [rbk.r(), CKVT.r(None)], [bk.r()])
                cp("dve", V1C.ap[:, kt, :, 0:64], bk.ap.rearrange("p (h d) -> p h d", h=8), [bk.r()], [V1C.r(kt)])
            nkeys = nkt * 128
            kslabs = [(0, 512), (512, 512)] + ([(1024, 256)] if sample else [])
            for hg in range(2):
                for hh in range(4):
                    h_ = hg * 4 + hh
                    for s in range(2):
                        bk = next_bank((0, 1))
                        for kc in range(3):
                            mm(bk.ap[0:96, :], slq[:, kc, h_ * 96:(h_ + 1) * 96], DQN.ap[:, kc, slab(s)], kc == 0, kc == 2,
                               [rbq.r(), DQN.r(s)], [bk.r()])
                        if not sample:
                            cp(alt_eng(), QC.ap[0:96, hh, slab(s)], bk.ap[0:96, :], [bk.r()], [QC.r(hh, s)])
                        else:
                            bk2 = next_bank((2, 3))
                            for kc in range(3):
                                mm(bk2.ap[64:96, :], WUQP.ap[:, kc, h_ * 32:(h_ + 1) * 32], DQN.ap[:, kc, slab(s)], kc == 0, kc == 2,
                                   [WUQP.r(), DQN.r(s)], [bk2.r()])
                            cp("act", QC.ap[0:64, hh, slab(s)], bk.ap[0:64, :], [bk.r()], [QC.r(hh, s)])
                            t0, t1 = TMP[0], TMP[1]
                            tt(t0.ap[64:96, :], bk.ap[64:96, :], COS.ap[64:96, slab(s)], ALU.mult, [bk.r(), COS.r()], [t0.r()])
                            tt(t1.ap[64:96, :], bk2.ap[64:96, :], SIN.ap[64:96, slab(s)], ALU.mult, [bk2.r(), SIN.r()], [t1.r()])
                            tt(QC.ap[64:96, hh, slab(s)], t0.ap[64:96, :], t1.ap[64:96, :], ALU.add, [t0.r(), t1.r()], [QC.r(hh, s)])
                    for (k0, kn) in kslabs:
                        bk = next_bank((0, 1))
                        for kc in range(2):
                            mm(bk.ap[0:64, 0:kn], slk[:, kc, h_ * 128:h_ * 128 + 64], CKVT.ap[:, kc, k0:k0 + kn], kc == 0, kc == 1,
                               [rbk.r(), CKVT.r(None)], [bk.r()])
                        cp(alt_eng(), KC.ap[0:64, hh, k0:k0 + kn], bk.ap[0:64, 0:kn], [bk.r()], [KC.r(hh, k0 // 512)])
                cp("dve", KC.ap[64:96, :, 0:nkeys], KRT.ap[64:96, 0:nkeys].unsqueeze(1).to_broadcast([32, 4, nkeys]),
                   [KRT.r(None)], [KC.r(None, None)])
                if sample:
                    def ktiles_c(t):
                        return [(kt, "p", None) for kt in range(10)]
                else:
                    def ktiles_c(t):
                        return [(2 * (t // 2), "p", None), (2 * (t // 2) + 1, "p", None)]

                def QhC(hh):
                    return QC.ap[0:96, hh, :], (lambda t, hh=hh: [QC.r(hh, t // 4)])

                def KhC(hh):
                    return KC.ap[0:96, hh, :], [KC.r(hh, None)]
                attention_c(u, l, G, QhC, KhC, ktiles_c, hg)
            itq.release()
            itk.release()
            it = acquire(("mz", u, l)); rb, sl = it.rb, it.sl
            gate_mul(G, rb, sl)
            it.release()

            if SUB <= 4:
                return
            if dbg is not None and (u, l) == unit_order[-1]:
                dsd = new_dsem()
                for k in range(3):
                    out_dmas.append(dma("pool", dsd, dbg["g"][k].rearrange("p (c n) -> p c n", c=4), GT[k].ap, [GT[k].r()], []))
                out_dmas.append(dma("pool", dsd, dbg["ht"].rearrange("p (c n) -> p c n", c=8), HT.ap, [HT.r()], []))
                out_dmas.append(dma("sp", new_dsem(), dbg["mod"], MODR.ap.rearrange("p l c j -> p (l c j)"), [MODR.r()], []))
            for half in range(2):
                for k in range(3):
                    if u == 1 and l == 0 and (1, 1) in unit_order:
                        etab_head(1, half * 3 + k)
                        if half == 1 and k == 2:
                            etab_head(1, 6)
                            etab_head(1, 7)
                    itg = acquire(("mg", u, l, k, half)); rbg, slg = itg.rb, itg.sl
                    itb = acquire(("wb", u, l, k, half)); rbb, slb = itb.rb, itb.sl
                    for cc in range(4):
                        c = half * 4 + cc
                        for s in range(2):
                            bkB = next_bank((0, 1, 4, 5))
                            for kc in range(4):
                                mm(bkB.ap, slb[:, kc, cc * 128:(cc + 1) * 128], GT[k].ap[:, kc, slab(s)], kc == 0, kc == 3,
                                   [rbb.r(), GT[k].r(s)], [bkB.r()])
                            bkG = next_bank((2, 3, 6, 7))
                            for kc in range(8):
                                mm(bkG.ap, slg[:, kc, cc * 128:(cc + 1) * 128], HT.ap[:, kc, slab(s)], kc == 0, kc == 7,
                                   [rbg.r(), HT.r(s)], [bkG.r()])
                            sg = SG[(cc * 2 + s) % 2]
                            act(sg.ap, bkG.ap, AF.Sigmoid, [bkG.r()], [sg.r()])
                            if k == 0:
                                tt(ACC.ap[:, cc, slab(s)], bkB.ap, sg.ap, ALU.mult, [bkB.r(), sg.r()], [ACC.r(cc, s)])
                            else:
                                tm = TMP[(cc * 2 + s) % 2]
                                tt(tm.ap, bkB.ap, sg.ap, ALU.mult, [bkB.r(), sg.r()], [tm.r()])
                                if k == 1:
                                    tt(ACC.ap[:, cc, slab(s)], ACC.ap[:, cc, slab(s)], tm.ap, ALU.add, [ACC.r(cc, s), tm.r()], [ACC.r(cc, s)])
                                else:
                                    tt(MT.ap[:, c, slab(s)], ACC.ap[:, cc, slab(s)], tm.ap, ALU.add, [ACC.r(cc, s), tm.r()], [MT.r(c, s)])
                    itg.release()
                    itb.release()
            if dbg is not None and (u, l) == unit_order[-1]:
                out_dmas.append(dma("pool", new_dsem(), dbg["mt"].rearrange("p (c n) -> p c n", c=8), MT.ap, [MT.r()], []))
            if defer_gate0 and (u, l) == (0, 0):
                mod_slab(0, 4)
                mod_slab(0, 5)
            itos = [acquire(("wo", u, l, half)) for half in range(2)]
            for s in range(2):
                for c in range(8):
                    ito = itos[c // 4]
                    rbo, slo = ito.rb, ito.sl
                    cc = c % 4
                    bk = next_bank()
                    for kc in range(8):
                        mm(bk.ap, slo[:, kc, cc * 128:(cc + 1) * 128], MT.ap[:, kc, slab(s)], kc == 0, kc == 7,
                           [rbo.r(), MT.r(None, s)], [bk.r()])
                    stt(XT.ap[:, c, slab(s)], bk.ap, modG(l, c, cond), XT.ap[:, c, slab(s)], ALU.mult, ALU.add,
                        [bk.r(), MODR.r(), XT.r(c, s)], [XT.r(c, s)])
            for ito in itos:
                ito.release()

        def attention_c(u, l, G, QhC, KhC, ktiles_c, hg):
            if u == 0:
                attention2(G, QhC, KhC, V1C, PTC4, YC, RDC, lambda t: [[2 * (t // 2), 2 * (t // 2) + 1]], 96.0 ** -0.5, hg)
            else:
                attention2(G, QhC, KhC, V1C, PTC4, YC, RDC, lambda t: [[0, 1, 2, 3], [4, 5, 6, 7], [8, 9]], 96.0 ** -0.5, hg)

        def attention_long(u, l, G, Qh, Kh, hg):
            scale = 96.0 ** -0.5
            steps = [(t, hh, gi) for t in range(8) for hh in range(4) for gi in range(2)]
            obank = BANK[0]
            tbank = BANK[1]

            def do_S(n):
                t, hh, gi = steps[n]
                sb_ = SBUF2[n % 3]
                qap, qregs = Qh(hh)
                kap, kregs = Kh(hh)
                for j in range(5):
                    kt = gi * 5 + j
                    mm(sb_.ap[:, j * 128:(j + 1) * 128], kap[:, kt * 128:(kt + 1) * 128], qap[:, t * 128:(t + 1) * 128],
                       True, True, kregs + qregs(t), [sb_.r()])
                pt = PTC[n % 3]
                act(pt.ap[:, 0:640], sb_.ap[:, 0:640], AF.Exp, [sb_.r()], [pt.r()], scale=scale)

            def do_PV(n):
                t, hh, gi = steps[n]
                h_ = hg * 4 + hh
                pt = PTC[n % 3]
                for j in range(5):
                    kt = gi * 5 + j
                    mm(obank.ap[:, hh * 65:(hh + 1) * 65], pt.ap[:, j * 128:(j + 1) * 128], V1C.ap[:, kt, h_, 0:65],
                       kt == 0, kt == 9, [pt.r(), V1C.r(kt)], [obank.r()])
                if hh == 3 and gi == 1:
                    y = YC[t % 2]
                    rd = RDC[t % 2]
                    o3 = obank.ap[:, 0:260].rearrange("p (h d) -> p h d", h=4)
                    P.op("dve", lambda h: h.reciprocal(out=rd.ap, in_=o3[:, :, 64]), [obank.r()], [rd.r()])
                    tt(y.ap.rearrange("p (h d) -> p h d", h=4), o3[:, :, 0:64], rd.ap.unsqueeze(2).to_broadcast([128, 4, 64]),
                       ALU.mult, [obank.r(), rd.r()], [y.r()])
                    for j in range(2):
                        tr(tbank.ap[:, j * 128:(j + 1) * 128], y.ap[:, j * 128:(j + 1) * 128], [y.r()], [tbank.r()])
                    for j in range(2):
                        cp("act", G.ap[:, hg * 2 + j, t * 128:(t + 1) * 128], tbank.ap[:, j * 128:(j + 1) * 128],
                           [tbank.r()], [G.r(t // 4)])

            do_S(0)
            do_S(1)
            for n in range(len(steps)):
                if n + 2 < len(steps):
                    do_S(n + 2)
                do_PV(n)

        for u in range(2):
            load_unit(u)
            if u == 0:
                mod_phase(0)
                if not interleave_mod1:
                    mod_phase(1)
            for l in range(2):
                if (u, l) in unit_order:
                    run_layer(u, l)
            if u == 0:
                prefetch_x(1)
            final_unit(u)

        fin = P.op("sp", lambda h: h.nop())
        fin.deps.update(out_dmas)
        stats = P.emit(block, sems)
    return nc, stats


def _static_tables():
    ident = np.eye(128, dtype=np.float32)
    pos = np.arange(1024)
    row = (pos // 64).astype(np.float32)
    col = (pos % 64).astype(np.float32)
    inv = (np.float32(10000.0) ** (-np.arange(8, dtype=np.float32) / np.float32(8))).astype(np.float32)
    ang = np.concatenate([row[:, None] * inv, col[:, None] * inv], axis=-1).astype(np.float32)
    cos, sin = np.cos(ang).astype(np.float32), np.sin(ang).astype(np.float32)
    cos2 = np.concatenate([cos, cos], axis=1).T
    sins = np.concatenate([-sin, sin], axis=1).T
    rope = np.ascontiguousarray(np.stack([cos2, sins], 0)).astype(np.float32)
    kc = np.arange(64)[:, None]
    c = np.arange(64)[None, :]
    lo = np.clip(c - 8, 0, 48)
    m = ((kc >= lo) & (kc < lo + 16)).astype(np.float32)
    cmask = np.ascontiguousarray(np.concatenate([m, m], 0))
    return ident, rope, cmask


_CACHE = {}


def kernel(x_prompt, x_sample, cache_na_k, cache_na_v, cache_mla_ckv, cache_mla_krope, c, c_ctx,
           norm_g, w_mod, b_mod, w_in, na_rpb, sgu_w, sgu_b, mla_q_norm, mla_w_uq, mla_kv_norm,
           mla_w_ukv, w_branch, w_out, final_norm_g, _stage=99):
    f = lambda a: np.ascontiguousarray(np.asarray(a, dtype=np.float32))
    if _stage not in _CACHE:
        _CACHE[_stage] = build_program(_stage)
    nc, stats = _CACHE[_stage]
    ident, rope, cmask = _static_tables()
    x_prompt, x_sample = f(x_prompt), f(x_sample)
    rpbpad = np.zeros((2, 8, 15, 128), np.float32)
    rpbpad[..., 48:79] = f(na_rpb)
    shared = {
        "norm_g": f(norm_g), "w_mod": f(w_mod), "b_mod": f(b_mod), "w_in": f(w_in), "rpbpad": rpbpad,
        "sgu_w": f(sgu_w), "sgu_b": f(sgu_b).reshape(2, 512), "q_norm": f(mla_q_norm), "w_uq": f(mla_w_uq),
        "kv_norm": f(mla_kv_norm), "w_ukv": f(mla_w_ukv), "w_branch": f(w_branch), "w_out": f(w_out),
        "fng": f(final_norm_g), "ident": ident, "rope": rope, "cmask": cmask,
    }
    c, c_ctx = f(c), f(c_ctx)
    cnk, cnv = f(cache_na_k).reshape(8, 2, 256, 512), f(cache_na_v).reshape(8, 2, 256, 512)
    cckv, ckr = f(cache_mla_ckv), f(cache_mla_krope)
    in_maps = []
    for i in range(NCORES):
        m = dict(shared)
        m["xp"] = x_prompt[4 * i:4 * i + 4].reshape(1024, 1024)
        m["xs"] = x_sample[i]
        m["cnk"], m["cnv"], m["cckv"], m["ckr"] = cnk[i], cnv[i], cckv[i], ckr[i]
        m["c2"] = np.ascontiguousarray(np.stack([c_ctx, c[i]], 0))
        in_maps.append(m)
    res = run_bass_kernel_spmd(nc, in_maps, core_ids=list(range(NCORES)))
    rs = res.results
    global _LAST
    _LAST = rs
    y_prompt = np.concatenate([r["yp"].reshape(4, 256, 1024) for r in rs], 0)
    y_sample = np.stack([r["ys"] for r in rs], 0)
    sk = np.concatenate([r["sk"].reshape(4, 2, 256, 8, 64) for r in rs], 0)
    sv = np.concatenate([r["sv"].reshape(4, 2, 256, 8, 64) for r in rs], 0)
    sckv = np.concatenate([r["sckv"] for r in rs], 0)
    skr = np.concatenate([r["skr"] for r in rs], 0)
    return (y_prompt.astype(np.float32), y_sample.astype(np.float32), sk.astype(np.float32), sv.astype(np.float32),
            sckv.astype(np.float32), skr.astype(np.float32))
```

```python
import numpy as np
from contextlib import ExitStack
import concourse.bass as bass
import concourse.mybir as mybir
from concourse.bass_utils import run_bass_kernel_spmd

F32 = mybir.dt.float32
BF16 = mybir.dt.bfloat16
AF = mybir.ActivationFunctionType
ALU = mybir.AluOpType

ENGS = ("pe", "act", "dve", "pool", "sp")
EPS = 1e-6
NCORES = 8
RING = 4
ARENA_BYTES = 90112


class Op:
    __slots__ = ("eng", "fn", "deps", "signals", "sigval", "dsem", "dval")

    def __init__(self, eng, fn):
        self.eng = eng
        self.fn = fn
        self.deps = set()
        self.signals = False
        self.sigval = 0
        self.dsem = None
        self.dval = 0


class DSem:
    def __init__(self, sem):
        self.sem = sem
        self.count = 0
        self.last = None


class Reg:
    __slots__ = ("space", "lo", "hi", "name", "lane", "last_w", "readers", "ov")


def _lanes_disjoint(a, b):
    for x, y in zip(a, b):
        if x is not None and y is not None and x != y:
            return True
    return False


class Prog:
    def __init__(self):
        self.ops = []
        self.eng_ops = {e: [] for e in ENGS}
        self.spaces = {}
        self.regs = {}

    def reg(self, space, lo, hi, name, lane=()):
        key = (name, lane)
        r = self.regs.get(key)
        if r is not None:
            return r
        r = Reg()
        r.space, r.lo, r.hi, r.name, r.lane = space, lo, hi, name, lane
        r.last_w = None
        r.readers = {}
        r.ov = [r]
        lst = self.spaces.setdefault(space, [])
        for q in lst:
            if q.lo < hi and lo < q.hi:
                if q.name == name and _lanes_disjoint(q.lane, lane):
                    continue
                q.ov.append(r)
                r.ov.append(q)
        lst.append(r)
        self.regs[key] = r
        return r

    def _track(self, o, reads, writes):
        deps = o.deps
        for r in reads:
            for q in r.ov:
                if q.last_w is not None:
                    deps.add(q.last_w)
        for r in writes:
            for q in r.ov:
                if q.last_w is not None:
                    deps.add(q.last_w)
                for k, x in q.readers.items():
                    if k == "dma":
                        deps.update(x)
                    else:
                        deps.add(x)
        for r in reads:
            if o.dsem is not None:
                r.readers.setdefault("dma", []).append(o)
            else:
                r.readers[o.eng] = o
        for r in writes:
            r.last_w = o
            r.readers = {}
        deps.discard(o)
        self.ops.append(o)
        self.eng_ops[o.eng].append(o)
        return o

    def op(self, eng, fn, reads=(), writes=()):
        return self._track(Op(eng, fn), reads, writes)

    def dma(self, eng, dsem, fn, reads=(), writes=()):
        o = Op(eng, fn)
        o.dsem = dsem
        dsem.count += 16
        o.dval = dsem.count
        if dsem.last is not None:
            o.deps.add(dsem.last)
        dsem.last = o
        return self._track(o, reads, writes)

    def emit(self, block, sems):
        for o in self.ops:
            for d in o.deps:
                if d.dsem is None:
                    if d.eng == "pe" and o.eng == "pe" and o.dsem is None:
                        continue
                    d.signals = True
        for e in ENGS:
            c = 0
            for o in self.eng_ops[e]:
                if o.signals:
                    c += 1
                    o.sigval = c
        blk = {"pe": block.tensor, "act": block.scalar, "dve": block.vector,
               "pool": block.gpsimd, "sp": block.sync}
        stats = {}
        for e in ENGS:
            ops = self.eng_ops[e]
            if not ops:
                continue
            nw = [0]

            def body(h, ops=ops, e=e, nw=nw):
                waited = {}
                for o in ops:
                    ws = {}
                    for d in o.deps:
                        if d.dsem is not None:
                            key = ("d", id(d.dsem))
                            sem, val = d.dsem.sem, d.dval
                        else:
                            if d.eng == "pe" and e == "pe" and o.dsem is None:
                                continue
                            key = ("e", d.eng)
                            sem, val = sems[d.eng], d.sigval
                        if waited.get(key, 0) >= val:
                            continue
                        if key not in ws or ws[key][1] < val:
                            ws[key] = (sem, val)
                    for key, (sem, val) in ws.items():
                        h.wait_ge(sem, val)
                        waited[key] = val
                        nw[0] += 1
                    ins = o.fn(h)
                    if o.dsem is not None:
                        ins.then_inc(o.dsem.sem, 16)
                    elif o.signals:
                        ins.then_inc(sems[e], 1)

            blk[e](body)
            stats[e] = (len(ops), nw[0])
        return stats


class Buf:
    def __init__(self, P, name, ap, space, lo, hi, nlane=0):
        self.P, self.name, self.ap, self.space, self.lo, self.hi, self.nlane = P, name, ap, space, lo, hi, nlane

    def r(self, *lane):
        lane = tuple(lane) + (None,) * (self.nlane - len(lane))
        return self.P.reg(self.space, self.lo, self.hi, self.name, lane)


def _na_row_lo(r):
    return min(max(r - 4, 0), 8)


def _na_tiles(t):
    us = []
    for u in range(7, -1, -1):
        quads = []
        anyv = False
        for a in range(2):
            for b in range(2):
                kr, r = 2 * u + a, 2 * t + b
                lo = _na_row_lo(r)
                v = lo <= kr <= lo + 7
                anyv = anyv or v
                if not v:
                    quads.append((a, b))
        if anyv:
            us.append((u, quads))
    return us


C_Q, C_K, C_V, C_Z = 0, 512, 1024, 1536
C_SU, C_SV, C_SZ = 2048, 2560, 3072
C_DQ, C_DKV, C_KR, C_MZ, C_MG = 3584, 3968, 4224, 4256, 4768


SUB = 99
SUB2 = 99
VENG = 'dve'


def build_program(stage=99):
    nc = bass.Bass("TRN2", target_bir_lowering=False)

    def din(name, shape):
        return nc.dram_tensor(name, list(shape), F32, kind="ExternalInput")

    def dout(name, shape):
        return nc.dram_tensor(name, list(shape), F32, kind="ExternalOutput")

    xin = [din("xp", (1024, 1024)).ap(), din("xs", (1024, 1024)).ap()]
    cnk = din("cnk", (2, 256, 512)).ap()
    cnv = din("cnv", (2, 256, 512)).ap()
    cckv = din("cckv", (2, 256, 256)).ap()
    ckr = din("ckr", (2, 256, 32)).ap()
    c2 = din("c2", (2, 1024))
    norm_g = din("norm_g", (2, 1024))
    w_mod = din("w_mod", (2, 1024, 3072)).ap()
    b_mod = din("b_mod", (2, 3072))
    w_in = din("w_in", (2, 1024, 7840)).ap()
    rpbpad = din("rpbpad", (2, 8, 15, 128))
    sgu_w = din("sgu_w", (2, 4, 128, 128)).ap()
    sgu_b = din("sgu_b", (2, 512))
    q_norm = din("q_norm", (2, 384))
    w_uq = din("w_uq", (2, 384, 768)).ap()
    kv_norm = din("kv_norm", (2, 256))
    w_ukv = din("w_ukv", (2, 256, 1024)).ap()
    w_branch = din("w_branch", (2, 3, 512, 1024)).ap()
    w_out = din("w_out", (2, 1024, 1024)).ap()
    fng = din("fng", (1024,))
    ident_d = din("ident", (128, 128)).ap()
    rope_d = din("rope", (2, 32, 1024)).ap()
    cmask_d = din("cmask", (128, 64)).ap()

    yout = [dout("yp", (1024, 1024)).ap(), dout("ys", (1024, 1024)).ap()]
    sk_o = dout("sk", (4, 2, 256, 512)).ap()
    sv_o = dout("sv", (4, 2, 256, 512)).ap()
    sckv_o = dout("sckv", (4, 2, 256, 256)).ap()
    skr_o = dout("skr", (4, 2, 256, 32)).ap()

    dbg = None
    if stage < 99:
        dbg = {"g": dout("dbg_g", (3, 128, 4096)).ap(), "mt": dout("dbg_mt", (128, 8192)).ap(),
               "ht": dout("dbg_ht", (128, 8192)).ap(), "mod": dout("dbg_mod", (128, 96)).ap()}
    P = Prog()
    with ExitStack() as es:
        es.enter_context(nc.allow_non_contiguous_dma(reason="small strided parameter loads"))

        def sbt(name, shape, dt):
            return es.enter_context(nc.sbuf_tensor(name, list(shape), dt))

        def pst(name, shape, dt):
            return es.enter_context(nc.psum_tensor(name, list(shape), dt))

        def pbuf(name, shape, dt, nlane=0):
            t = sbt(name, shape, dt)
            return Buf(P, name, t[:], name, 0, 1, nlane)

        XT = pbuf("XT", (128, 8, 1024), F32, 2)
        HT = pbuf("HT", (128, 8, 1024), BF16, 2)
        GT = [pbuf(f"G{k}", (128, 4, 1024), BF16, 1) for k in range(3)]
        ring = [pbuf(f"ring{i}", (128, 4096), BF16) for i in range(RING)]
        IDT = pbuf("ident_sb", (128, 128), F32)
        ONES = pbuf("ones_sb", (128, 128), F32)
        CMASK = pbuf("cmask_sb", (128, 64), F32)
        CT = pbuf("condT", (128, 2, 8), F32)
        CTB = pbuf("condTb", (128, 2, 8), BF16)
        MODR = pbuf("modr", (128, 2, 24, 2), F32)
        MODA = pbuf("moda", (128, 2, 8, 2), F32)
        BMODT = pbuf("bmodT", (128, 2, 24), F32)
        GNT = pbuf("gnT", (128, 2, 8), F32)
        FNGT = pbuf("fngT", (128, 8), F32)
        QNT = pbuf("qnT", (128, 2, 3), F32)
        KVNT = pbuf("kvnT", (128, 2, 2), F32)
        WST = pbuf("wsT", (128, 2, 4, 128), BF16)
        BSB = pbuf("bsb", (128, 2, 512), F32)
        KVNB = pbuf("kvnb", (128, 2, 256), F32)
        WKRP = pbuf("wkrp", (128, 8, 32), BF16)
        WUQP = pbuf("wuqp", (128, 3, 256), BF16)
        SMALL = pbuf("small", (128, 64), F32, 1)
        HMASK = pbuf("hmask", (128, 16), F32)
        ONES512 = pbuf("ones512", (128, 512), F32)

        arena_t = sbt("arena", (128, ARENA_BYTES // 4), F32)

        def aview(name, off, shape, dt, nlane=0):
            esz = 4 if dt == F32 else 2
            n = 1
            for s in shape[1:]:
                n *= s
            nbytes = n * esz
            assert off % 4 == 0 and nbytes % 4 == 0 and off + nbytes <= ARENA_BYTES, (name, off, nbytes)
            ap = arena_t[:, off // 4:(off + nbytes) // 4]
            if dt != F32:
                ap = ap.bitcast(dt)
            if len(shape) == 3:
                ap = ap.rearrange("p (a b) -> p a b", a=shape[1])
            elif len(shape) == 4:
                ap = ap.rearrange("p (a b c) -> p a b c", a=shape[1], b=shape[2])
            return Buf(P, name, ap, "arena", off, off + nbytes, nlane)

        SQ = [aview(f"SQ{i}", i * 2048, (128, 512), F32) for i in range(2)]
        RSTD = aview("RSTD", 4096, (128, 512), F32)
        TMP = [aview(f"TMP{i}", 6144 + i * 2048, (128, 512), F32) for i in range(2)]
        B0 = 10240
        SQN = [aview(f"SQN{i}", B0 + i * 2048, (128, 512), F32) for i in range(8)]
        RS2 = [RSTD, aview("RSTD2", 0, (128, 512), F32)]
        XS = [aview(f"XS{i}", B0 + i * 4096, (128, 1024), F32) for i in range(2)]
        XSL = XS + [aview(f"XS{i}", B0 + i * 4096, (128, 1024), F32) for i in range(2, 4)]
        YT = aview("YT", B0 + 8192, (128, 8, 512), F32, 1)
        XSP = [aview(f"XSP{i}", B0 + 24576 + i * 4096, (128, 1024), F32) for i in range(3)]
        QA = aview("QA", B0, (128, 4, 1024), BF16, 2)
        QA1 = aview("QA1", B0 + 65632, (128, 4, 1024), BF16, 2)
        KA = aview("KA", B0 + 8192, (128, 4, 1280), BF16, 2)
        VA1 = aview("VA1", B0 + 18432, (128, 10, 8, 66), BF16, 1)
        PTA = [aview(f"PTA{i}", B0 + 28992 + i * 1792, (128, 896), BF16) for i in range(2)] + [aview("PTA2", B0 + 63840, (128, 896), BF16)]
        PTA4 = PTA + [aview("PTA3", B0 + 73824, (128, 896), BF16)]
        YA = [aview(f"YA{i}", B0 + 32576 + i * 1024, (128, 256), F32) for i in range(2)]
        RDA = [aview(f"RDA{i}", B0 + 34624 + i * 16, (128, 4), F32) for i in range(2)]
        OSTA = [aview(f"OSTA{i}", B0 + 34656 + i * 2048, (128, 512), F32) for i in range(2)]
        CSTA = aview("CSTA", B0 + 34656, (128, 2, 512), F32)
        ERAW = [aview(f"ERAW{i}", B0 + 38752 + i * 3584, (128, 14, 64), F32) for i in range(2)]
        EEXP = aview("EEXP", B0 + 45920, (128, 14, 64), F32)
        ETAB = aview("ETAB", B0 + 49504, (128, 8, 14, 64), BF16, 1)
        ETAB2 = aview("ETAB2", B0 + 38752, (128, 8, 10, 64), BF16)
        SU = aview("SU", B0, (128, 4, 1024), BF16, 1)
        VB = aview("VB", B0 + 8192, (128, 8, 512), BF16, 1)
        GV = [aview(f"GV{i}", B0 + 16384 + i * 2048, (128, 512), F32) for i in range(3)]
        DQ = aview("DQ", B0, (128, 3, 512), F32, 1)
        DKV = aview("DKV", B0 + 6144, (128, 2, 512), F32, 1)
        DQN = aview("DQN", B0 + 10240, (128, 3, 1024), BF16, 1)
        CKVT = aview("CKVT", B0 + 16384, (128, 2, 1280), BF16, 1)
        KRT = aview("KRT", B0 + 21504, (128, 1280), BF16, 1)
        QC = aview("QC", B0 + 24064, (128, 4, 1024), BF16, 2)
        KC = aview("KC", B0 + 32256, (128, 4, 1280), BF16, 2)
        V1C = aview("V1C", B0 + 42496, (128, 10, 8, 66), BF16, 1)
        PTC = [aview(f"PTC{i}", B0 + 53056 + i * 1792, (128, 896), BF16) for i in range(2)] + [aview("PTC2", B0 + 72032, (128, 896), BF16)]
        PTC4 = PTC + [aview("PTC3", B0 + 73824, (128, 896), BF16)]
        YC = [aview(f"YC{i}", B0 + 56640 + i * 1024, (128, 256), F32) for i in range(2)]
        RDC = [aview(f"RDC{i}", B0 + 58688 + i * 16, (128, 4), F32) for i in range(2)]
        OSTC = [aview(f"OSTC{i}", B0 + 58720 + i * 1152, (128, 288), F32) for i in range(2)]
        CSTC = aview("CSTC", B0 + 61024, (128, 2, 256), F32)
        CSTK = aview("CSTK", B0 + 63072, (128, 2, 96), F32)
        COS = aview("COS", B0 + 63840, (128, 1024), F32)
        SIN = aview("SIN", B0 + 67936, (128, 1024), F32)
        ACC = aview("ACC", B0, (128, 4, 1024), F32, 2)
        MT = aview("MT", B0 + 16384, (128, 8, 1024), BF16, 2)
        SG = [aview(f"SG{i}", B0 + 32768 + i * 2048, (128, 512), F32) for i in range(2)]

        ps2 = [pst(f"ps{i}", (128, 1024), F32) for i in range(4)]

        def bank(b):
            return Buf(P, f"bank{b}", ps2[b // 2][:, (b % 2) * 512:(b % 2 + 1) * 512], "psum", b * 2048, (b + 1) * 2048)

        BANK = [bank(b) for b in range(8)]
        SBUF2 = [Buf(P, f"sbank{i}", ps2[1 + i][:], "psum", (2 + 2 * i) * 2048, (4 + 2 * i) * 2048) for i in range(3)]
        XBANK2 = Buf(P, "xbank2", ps2[0][:], "psum", 0, 4096)

        sems = {e: es.enter_context(nc.semaphore("s_" + e)) for e in ENGS}
        _nds = [0]

        def new_dsem():
            _nds[0] += 1
            return DSem(es.enter_context(nc.semaphore(f"d{_nds[0]}")))

        block = es.enter_context(nc.Block())

        def mm(out, lhsT, rhs, start, stop, reads, writes):
            P.op("pe", lambda h: h.matmul(out, lhsT=lhsT, rhs=rhs, start=start, stop=stop), reads, writes)

        def tr(out, in_, reads, writes):
            P.op("pe", lambda h: h.transpose(out, in_, IDT.ap), list(reads) + [IDT.r()], writes)

        def act(out, in_, func, reads, writes, scale=None, bias=None):
            kw = {}
            if scale is not None:
                kw["scale"] = scale
            if bias is not None:
                kw["bias"] = bias
            P.op("act", lambda h: h.activation(out=out, in_=in_, func=func, **kw), reads, writes)

        def tt(out, in0, in1, op, reads, writes, eng="dve"):
            P.op(eng, lambda h: h.tensor_tensor(out=out, in0=in0, in1=in1, op=op), reads, writes)

        def ts(out, in0, s1, s2, op0, op1, reads, writes):
            P.op("dve", lambda h: h.tensor_scalar(out=out, in0=in0, scalar1=s1, scalar2=s2, op0=op0, op1=op1), reads, writes)

        def stt(out, in0, scalar, in1, op0, op1, reads, writes):
            P.op("dve", lambda h: h.scalar_tensor_tensor(out=out, in0=in0, scalar=scalar, in1=in1, op0=op0, op1=op1), reads, writes)

        def cp(eng, out, in_, reads, writes):
            if eng == "act":
                P.op("act", lambda h: h.activation(out=out, in_=in_, func=AF.Copy), reads, writes)
            else:
                P.op("dve", lambda h: h.tensor_copy(out=out, in_=in_), reads, writes)

        def dma(eng, ds, out, in_, reads, writes):
            return P.dma(eng, ds, lambda h: h.dma_start(out=out, in_=in_), reads, writes)

        _cpi = [0]

        def alt_eng():
            _cpi[0] += 1
            return "act" if _cpi[0] % 2 else "dve"

        _pb = {}

        def next_bank(lst=(0, 1, 2, 3)):
            _pb[lst] = _pb.get(lst, -1) + 1
            return BANK[lst[_pb[lst] % len(lst)]]

        out_dmas = []

        items = []

        def win_src(l, c0, n):
            return w_in[l].rearrange("(kc p) n -> p kc n", p=128)[:, :, c0:c0 + n]

        def slab_dst(nk, n):
            return lambda rb: rb.ap[:, 0:nk * n].rearrange("p (k n) -> p k n", k=nk)

        def mod_items(l):
            for j in range(6):
                items.append((("mod", l, j), [(slab_dst(8, 512),
                                               w_mod[l].rearrange("(kc p) n -> p kc n", p=128)[:, :, j * 512:(j + 1) * 512])]))
        unit_order = [(0, 0), (0, 1), (1, 0), (1, 1)][:min(stage, 4)]
        mod_items(0)
        MOD1_AFTER = {"q": 0, "k": 1, "v": 2, "z": 3, "su": 4, "sv": 5}
        interleave_mod1 = len(unit_order) >= 2
        if not interleave_mod1:
            mod_items(1)
        for (u, l) in unit_order:
            for nm, c0 in (("q", C_Q), ("k", C_K), ("v", C_V), ("z", C_Z), ("su", C_SU), ("sv", C_SV), ("sz", C_SZ)):
                items.append(((nm, u, l), [(slab_dst(8, 512), win_src(l, c0, 512))]))
                if interleave_mod1 and (u, l) == (0, 0) and nm in MOD1_AFTER:
                    j = MOD1_AFTER[nm]
                    items.append((("mod", 1, j), [(slab_dst(8, 512),
                                                   w_mod[1].rearrange("(kc p) n -> p kc n", p=128)[:, :, j * 512:(j + 1) * 512])]))
            items.append((("dq", u, l), [(slab_dst(8, 384), win_src(l, C_DQ, 384))]))
            items.append((("dkvkr", u, l), [(slab_dst(8, 288), win_src(l, C_DKV, 288))]))
            items.append((("wuq", u, l), [(slab_dst(3, 768), w_uq[l].rearrange("(kc p) n -> p kc n", p=128))]))
            items.append((("wukv", u, l), [(slab_dst(2, 1024), w_ukv[l].rearrange("(kc p) n -> p kc n", p=128))]))
            items.append((("mz", u, l), [(slab_dst(8, 512), win_src(l, C_MZ, 512))]))
            for half in range(2):
                for k in range(3):
                    items.append((("mg", u, l, k, half), [(slab_dst(8, 512), win_src(l, C_MG + k * 1024 + half * 512, 512))]))
                    items.append((("wb", u, l, k, half), [(slab_dst(4, 512),
                                                           w_branch[l, k].rearrange("(kc p) n -> p kc n", p=128)[:, :, half * 512:(half + 1) * 512])]))
            for half in range(2):
                items.append((("wo", u, l, half), [(slab_dst(8, 512),
                                                    w_out[l].rearrange("(kc p) n -> p kc n", p=128)[:, :, half * 512:(half + 1) * 512])]))
        ring_ds = [new_dsem() for _ in range(RING)]
        wstate = {"issued": 0, "next": 0}
        wdone = [False] * len(items)

        def _pump():
            while wstate["issued"] < len(items):
                i = wstate["issued"]
                if i >= RING and not wdone[i - RING]:
                    break
                rb = ring[i % RING]
                for dstf, src in items[i][1]:
                    dma("pool", ring_ds[i % RING], dstf(rb), src, [], [rb.r()])
                wstate["issued"] = i + 1

        class _Item:
            def __init__(self, j):
                self.j = j
                self.rb = ring[j % RING]
                self.sl = items[j][1][0][0](self.rb)

            def release(self):
                wdone[self.j] = True
                _pump()
                key = items[self.j][0]
                if interleave_mod1 and len(key) == 3 and key[1:] == (0, 0) and key[0] in MOD1_AFTER:
                    mod_slab(1, MOD1_AFTER[key[0]])
                    if key[0] == "sv":
                        mod_finish(1)

        def acquire(key):
            j = wstate["next"]
            while items[j][0] != key and SUB < 99:
                wdone[j] = True
                j += 1
            assert items[j][0] == key, (items[j][0], key)
            wstate["next"] = j + 1
            _pump()
            assert wstate["issued"] > j, (j, key)
            return _Item(j)

        ds_c = [new_dsem() for _ in range(6)]
        dma("sp", ds_c[0], IDT.ap, ident_d, [], [IDT.r()])
        dma("sp", ds_c[1], CMASK.ap, cmask_d, [], [CMASK.r()])
        P.op("dve", lambda h: h.memset(ONES.ap, 1.0), [], [ONES.r()])
        P.op("dve", lambda h: h.memset(ONES512.ap, 1.0), [], [ONES512.r()])
        P.op("dve", lambda h: h.memset(HMASK.ap[0:64, 0:8], 1.0), [], [HMASK.r()])
        P.op("dve", lambda h: h.memset(HMASK.ap[64:128, 0:8], 0.0), [], [HMASK.r()])
        P.op("dve", lambda h: h.memset(HMASK.ap[0:64, 8:16], 0.0), [], [HMASK.r()])
        P.op("dve", lambda h: h.memset(HMASK.ap[64:128, 8:16], 1.0), [], [HMASK.r()])
        for j in range(2):
            dma("act", ds_c[2], CT.ap[:, j, :], bass.AP(tensor=c2, offset=j * 1024, ap=[[1, 128], [128, 8]]), [], [CT.r()])
        for l in range(2):
            dma("act", ds_c[3], BMODT.ap[:, l, :], bass.AP(tensor=b_mod, offset=l * 3072, ap=[[1, 128], [128, 24]]), [], [BMODT.r()])
            dma("act", ds_c[3], GNT.ap[:, l, :], bass.AP(tensor=norm_g, offset=l * 1024, ap=[[1, 128], [128, 8]]), [], [GNT.r()])
            dma("act", ds_c[4], QNT.ap[:, l, :], bass.AP(tensor=q_norm, offset=l * 384, ap=[[1, 128], [128, 3]]), [], [QNT.r()])
            dma("act", ds_c[4], KVNT.ap[:, l, :], bass.AP(tensor=kv_norm, offset=l * 256, ap=[[1, 128], [128, 2]]), [], [KVNT.r()])
            dma("act", ds_c[5], BSB.ap[:, l, :], bass.AP(tensor=sgu_b, offset=l * 512, ap=[[0, 128], [1, 512]]), [], [BSB.r()])
            dma("act", ds_c[5], KVNB.ap[:, l, :], bass.AP(tensor=kv_norm, offset=l * 256, ap=[[0, 128], [1, 256]]), [], [KVNB.r()])
        dma("act", ds_c[4], FNGT.ap, bass.AP(tensor=fng, offset=0, ap=[[1, 128], [128, 8]]), [], [FNGT.r()])
        act(CTB.ap, CT.ap, AF.Silu, [CT.r()], [CTB.r()])
        ds_w = new_dsem()
        for l in range(2):
            stg = XS[l]
            dma("sp", ds_w, stg.ap[:, 0:512].rearrange("p (g q) -> p g q", g=4), sgu_w[l].rearrange("g p q -> p g q"), [], [stg.r()])
            bk = next_bank()
            for g in range(4):
                tr(bk.ap[:, g * 128:(g + 1) * 128], stg.ap[:, g * 128:(g + 1) * 128], [stg.r()], [bk.r()])
            cp("dve", WST.ap[:, l].rearrange("p g q -> p (g q)"), bk.ap, [bk.r()], [WST.r()])

        def mod_slab(l, j):
            it = acquire(("mod", l, j))
            rb, sl = it.rb, it.sl
            pm = next_bank()
            for cc in range(4):
                for kc in range(8):
                    mm(pm.ap[:, cc * 2:cc * 2 + 2], sl[:, kc, cc * 128:(cc + 1) * 128], CTB.ap[:, :, kc],
                       kc == 0, kc == 7, [rb.r(), CTB.r()], [pm.r()])
            tt(MODR.ap[:, l, j * 4:(j + 1) * 4, :], pm.ap[:, 0:8].rearrange("p (c j) -> p c j", j=2),
               BMODT.ap[:, l, j * 4:(j + 1) * 4].unsqueeze(2).to_broadcast([128, 4, 2]), ALU.add,
               [pm.r(), BMODT.r()], [MODR.r()])
            it.release()

        def mod_finish(l):
            stt(MODA.ap[:, l], MODR.ap[:, l, 8:16, :], 1.0, GNT.ap[:, l, :].unsqueeze(2).to_broadcast([128, 8, 2]),
                ALU.add, ALU.mult, [MODR.r(), GNT.r()], [MODA.r()])

        def mod_phase(l):
            for j in range(6):
                mod_slab(l, j)
            mod_finish(l)

        def modA(l, c, cond):
            return MODA.ap[:, l, c, cond:cond + 1]

        def modS(l, c, cond):
            return MODR.ap[:, l, c, cond:cond + 1]

        def modG(l, c, cond):
            return MODR.ap[:, l, 16 + c, cond:cond + 1]

        def slab(s):
            return slice(s * 512, (s + 1) * 512)

        def rms_stats(src_ap_fn, src_regs_fn, nchunks, dim, bk):
            for c in range(nchunks):
                sq = SQ[c % 2]
                act(sq.ap, src_ap_fn(c), AF.Square, src_regs_fn(c), [sq.r()])
                mm(bk.ap, ONES.ap, sq.ap, c == 0, c == nchunks - 1, [ONES.r(), sq.r()], [bk.r()])
            act(RSTD.ap, bk.ap, AF.Ln, [bk.r()], [RSTD.r()], scale=1.0 / dim, bias=EPS)
            act(RSTD.ap, RSTD.ap, AF.Exp, [RSTD.r()], [RSTD.r()], scale=-0.5)

        def norm_phase(l, cond):
            bks = [next_bank(), next_bank()]
            for s in range(2):
                for c in range(8):
                    sq = SQN[c]
                    act(sq.ap, XT.ap[:, c, slab(s)], AF.Square, [XT.r(c, s)], [sq.r()])
                    mm(bks[s].ap, ONES.ap, sq.ap, c == 0, c == 7, [ONES.r(), sq.r()], [bks[s].r()])
            for s in range(2):
                rs = RS2[s]
                act(rs.ap, bks[s].ap, AF.Ln, [bks[s].r()], [rs.r()], scale=1.0 / 1024, bias=EPS)
            for s in range(2):
                rs = RS2[s]
                act(rs.ap, rs.ap, AF.Exp, [rs.r()], [rs.r()], scale=-0.5)
            for s in range(2):
                rs = RS2[s]
                for c in range(8):
                    tm = TMP[c % 2]
                    tt(tm.ap, XT.ap[:, c, slab(s)], rs.ap, ALU.mult, [XT.r(c, s), rs.r()], [tm.r()])
                    act(HT.ap[:, c, slab(s)], tm.ap, AF.Identity, [tm.r(), MODA.r(), MODR.r()], [HT.r(s, c)],
                        scale=modA(l, c, cond), bias=modS(l, c, cond))

        def projT(rb, sl, col0, m, src, nk, evac, prow=0, n_of=None):
            for s in range(2):
                bk = next_bank()
                for kc in range(nk):
                    mm(bk.ap[prow:prow + m, :], sl[:, kc, col0:col0 + m], src.ap[:, kc, slab(s)],
                       kc == 0, kc == nk - 1, [rb.r(), src.r(s)], [bk.r()])
                evac(s, bk)

        etab_built = set()
        etab_ds = []

        def etab_head(l, h_):
            if (l, h_) in etab_built:
                return
            etab_built.add((l, h_))
            if not etab_ds:
                etab_ds.extend([new_dsem(), new_dsem()])
            er = ERAW[h_ % 2]
            for a in range(2):
                src = bass.AP(tensor=rpbpad, offset=((l * 8 + h_) * 15 + a + 13) * 128,
                              ap=[[1, 64], [-128, 14], [1, 64]])
                dma("sp", etab_ds[h_ % 2], er.ap[a * 64:(a + 1) * 64], src, [], [er.r()])
            er_rev = bass.AP(tensor=er.ap.tensor, offset=er.ap.offset + 63,
                             ap=[list(er.ap.ap[0]), list(er.ap.ap[1]), [-1, 64]])
            act(EEXP.ap, er_rev, AF.Exp, [er.r()], [EEXP.r()])
            tt(ETAB.ap[:, h_], EEXP.ap, CMASK.ap.unsqueeze(1).to_broadcast([128, 14, 64]), ALU.mult,
               [EEXP.r(), CMASK.r()], [ETAB.r(h_)])
            if h_ == 7:
                cp("dve", ETAB2.ap, ETAB.ap[:, :, 2:12, :], [ETAB.r()], [ETAB2.r()])
                P.op("dve", lambda h: h.memset(ETAB2.ap[0:64, :, 0, :], 0.0), [], [ETAB2.r()])
                P.op("dve", lambda h: h.memset(ETAB2.ap[0:64, :, 9, :], 0.0), [], [ETAB2.r()])
                P.op("dve", lambda h: h.memset(ETAB2.ap[64:128, :, 0:2, :], 0.0), [], [ETAB2.r()])

        prefetched_x = set()

        def prefetch_x(u):
            dsp = [new_dsem() for _ in range(3)]
            for t in range(3):
                dma("sp", dsp[t], XSP[t].ap, xin[u][t * 128:(t + 1) * 128, :], [], [XSP[t].r()])
                prefetched_x.add((u, t))

        def load_unit(u):
            ds_x = [new_dsem() for _ in range(4)]
            for t in range(8):
                if (u, t) in prefetched_x:
                    xs = XSP[t]
                else:
                    xs = XSL[t % 4]
                    dma("sp", ds_x[t % 4], xs.ap, xin[u][t * 128:(t + 1) * 128, :], [], [xs.r()])
                if u == 1 and (1, 0) in unit_order:
                    etab_head(0, t)
                for hb in range(2):
                    bk = BANK[(2 * t + hb) % 4]
                    for j in range(4):
                        c = hb * 4 + j
                        tr(bk.ap[:, j * 128:(j + 1) * 128], xs.ap[:, c * 128:(c + 1) * 128], [xs.r()], [bk.r()])
                    cp(alt_eng(), XT.ap[:, hb * 4:hb * 4 + 4, t * 128:(t + 1) * 128],
                       bk.ap.rearrange("p (c n) -> p c n", c=4), [bk.r()],
                       [XT.r(hb * 4 + j, t // 4) for j in range(4)])

        def final_unit(u):
            ds_y = [new_dsem(), new_dsem()]
            for s in range(2):
                bk = next_bank()
                rms_stats(lambda c: XT.ap[:, c, slab(s)], lambda c: [XT.r(c, s)], 8, 1024, bk)
                for c in range(8):
                    tm = TMP[c % 2]
                    tt(tm.ap, XT.ap[:, c, slab(s)], RSTD.ap, ALU.mult, [XT.r(c, s), RSTD.r()], [tm.r()])
                    act(YT.ap[:, c, :], tm.ap, AF.Identity, [tm.r(), FNGT.r()], [YT.r(c)], scale=FNGT.ap[:, c:c + 1])
                for tq in range(4):
                    t = s * 4 + tq
                    xs = XS[t % 2]
                    for hb in range(2):
                        bk2 = BANK[(2 * t + hb) % 4]
                        for j in range(4):
                            c = hb * 4 + j
                            tr(bk2.ap[:, j * 128:(j + 1) * 128], YT.ap[:, c, tq * 128:(tq + 1) * 128], [YT.r(c)], [bk2.r()])
                        cp(alt_eng(), xs.ap[:, hb * 512:(hb + 1) * 512], bk2.ap, [bk2.r()], [xs.r()])
                    out_dmas.append(dma("sp", ds_y[t % 2], yout[u][t * 128:(t + 1) * 128, :], xs.ap, [xs.r()], []))

        def attention(u, mixer, l, Gk, Qh, Kh, V1, PT, Y, RD, key_tiles, scale, hg, etab=None):
            steps = []
            for t in range(8):
                for hh in range(4):
                    steps.append((t, hh))
            obank = BANK[0]
            tbank = BANK[1]

            def do_S(n):
                t, hh = steps[n]
                tiles = key_tiles(t)
                sb_ = SBUF2[n % 3]
                qap, qregs = Qh(hh)
                kap, kregs = Kh(hh)
                for j, (kt, kind, extra) in enumerate(tiles):
                    mm(sb_.ap[:, j * 128:(j + 1) * 128], kap[:, kt * 128:(kt + 1) * 128], qap[:, t * 128:(t + 1) * 128],
                       True, True, kregs + qregs(t), [sb_.r()])

            def do_exp(n):
                t, hh = steps[n]
                h_ = hg * 4 + hh
                tiles = key_tiles(t)
                nt = len(tiles)
                sb_ = SBUF2[n % 3]
                pt = PT[n % 3]
                act(pt.ap[:, 0:nt * 128], sb_.ap[:, 0:nt * 128], AF.Exp, [sb_.r()], [pt.r()], scale=scale)
                mj = [j for j, x in enumerate(tiles) if x[1] == "m"]
                if mj:
                    j0 = mj[0]
                    nu = len(mj)
                    u0 = tiles[j0][2][0]
                    i0 = 7 - 2 * (u0 - t) - 1
                    interior = any(tiles[j][2][1] for j in mj)
                    if interior:
                        assert nu == 5 and i0 == 2, (t, nu, i0)
                        tab, tab_r = ETAB2.ap[:, h_, 0:10, :], ETAB2.r()
                    else:
                        tab, tab_r = etab.ap[:, h_, i0:i0 + 2 * nu, :], etab.r(h_)
                    tt(pt.ap[:, j0 * 128:(j0 + nu) * 128], pt.ap[:, j0 * 128:(j0 + nu) * 128],
                       tab.rearrange("p i c -> p (i c)"), ALU.mult, [pt.r(), tab_r], [pt.r()])

            def do_PV(n):
                t, hh = steps[n]
                h_ = hg * 4 + hh
                tiles = key_tiles(t)
                nt = len(tiles)
                pt = PT[n % 3]
                for j, (kt, kind, extra) in enumerate(tiles):
                    mm(obank.ap[:, hh * 65:(hh + 1) * 65], pt.ap[:, j * 128:(j + 1) * 128], V1.ap[:, kt, h_, 0:65],
                       j == 0, j == nt - 1, [pt.r(), V1.r(kt)], [obank.r()])
                if hh == 3:
                    y = Y[t % 2]
                    rd = RD[t % 2]
                    o3 = obank.ap[:, 0:260].rearrange("p (h d) -> p h d", h=4)
                    P.op("dve", lambda h: h.reciprocal(out=rd.ap, in_=o3[:, :, 64]), [obank.r()], [rd.r()])
                    tt(y.ap.rearrange("p (h d) -> p h d", h=4), o3[:, :, 0:64], rd.ap.unsqueeze(2).to_broadcast([128, 4, 64]),
                       ALU.mult, [obank.r(), rd.r()], [y.r()])
                    for j in range(2):
                        tr(tbank.ap[:, j * 128:(j + 1) * 128], y.ap[:, j * 128:(j + 1) * 128], [y.r()], [tbank.r()])
                    for j in range(2):
                        cp("dve", Gk.ap[:, hg * 2 + j, t * 128:(t + 1) * 128], tbank.ap[:, j * 128:(j + 1) * 128],
                           [tbank.r()], [Gk.r(t // 4)])

            do_S(0)
            do_exp(0)
            do_S(1)
            do_exp(1)
            for n in range(len(steps)):
                if n + 2 < len(steps):
                    do_S(n + 2)
                    do_exp(n + 2)
                do_PV(n)

        def attention2(Gk, Qh, Kh, V1, PT4, Y, RD, groups_fn, scale, hg, etab=None):
            steps = [(t, hh, gi) for t in range(8) for hh in range(4) for gi in range(len(groups_fn(t)))]
            R = 3

            def do_S(n):
                t, hh, gi = steps[n]
                grp = groups_fn(t)[gi]
                sb_ = BANK[4 + n % 4]
                pt = PT4[n % 4]
                qap, qregs = Qh(hh)
                kap, kregs = Kh(hh)
                for j, ent in enumerate(grp):
                    kt = ent[0] if isinstance(ent, tuple) else ent
                    mm(sb_.ap[:, j * 128:(j + 1) * 128], kap[:, kt * 128:(kt + 1) * 128], qap[:, t * 128:(t + 1) * 128],
                       True, True, kregs + qregs(t), [sb_.r()])
                w = len(grp) * 128
                act(pt.ap[:, 0:w], sb_.ap[:, 0:w], AF.Exp, [sb_.r()], [pt.r()], scale=scale)
                mj = [j for j, x in enumerate(grp) if isinstance(x, tuple) and x[1] == "m"]
                if mj:
                    h_ = hg * 4 + hh
                    j0 = mj[0]
                    nu = len(mj)
                    u0 = grp[j0][2][0]
                    i0 = 7 - 2 * (u0 - t) - 1
                    tt(pt.ap[:, j0 * 128:(j0 + nu) * 128], pt.ap[:, j0 * 128:(j0 + nu) * 128],
                       etab.ap[:, h_, i0:i0 + 2 * nu, :].rearrange("p i c -> p (i c)"), ALU.mult,
                       [pt.r(), etab.r(h_)], [pt.r()])
                    for j in mj:
                        for (a, b) in grp[j][2][1]:
                            P.op("dve", lambda h, j=j, a=a, b=b, pt=pt: h.memset(pt.ap[a * 64:(a + 1) * 64, j * 128 + b * 64:j * 128 + (b + 1) * 64], 0.0),
                                 [], [pt.r()])

            def do_PV(n):
                t, hh, gi = steps[n]
                grps = groups_fn(t)
                grp = grps[gi]
                h_ = hg * 4 + hh
                ob = BANK[t % 2]
                pt = PT4[n % 4]
                for j, ent in enumerate(grp):
                    kt = ent[0] if isinstance(ent, tuple) else ent
                    mm(ob.ap[:, hh * 65:(hh + 1) * 65], pt.ap[:, j * 128:(j + 1) * 128], V1.ap[:, kt, h_, 0:65],
                       gi == 0 and j == 0, gi == len(grps) - 1 and j == len(grp) - 1, [pt.r(), V1.r(kt)], [ob.r()])

            def fin(t):
                ob = BANK[t % 2]
                tb = BANK[2 + t % 2]
                y = Y[t % 2]
                rd = RD[t % 2]
                o3 = ob.ap[:, 0:260].rearrange("p (h d) -> p h d", h=4)
                P.op("dve", lambda h: h.reciprocal(out=rd.ap, in_=o3[:, :, 64]), [ob.r()], [rd.r()])
                tt(y.ap.rearrange("p (h d) -> p h d", h=4), o3[:, :, 0:64], rd.ap.unsqueeze(2).to_broadcast([128, 4, 64]),
                   ALU.mult, [ob.r(), rd.r()], [y.r()])
                for j in range(2):
                    tr(tb.ap[:, j * 128:(j + 1) * 128], y.ap[:, j * 128:(j + 1) * 128], [y.r()], [tb.r()])
                for j in range(2):
                    cp("dve", Gk.ap[:, hg * 2 + j, t * 128:(t + 1) * 128], tb.ap[:, j * 128:(j + 1) * 128],
                       [tb.r()], [Gk.r(t // 4)])

            pending = []
            for n in range(min(R, len(steps))):
                do_S(n)
            for n in range(len(steps)):
                if n + R < len(steps):
                    do_S(n + R)
                do_PV(n)
                t, hh, gi = steps[n]
                if hh == 3 and gi == len(groups_fn(t)) - 1:
                    pending.append((n + 2, t))
                while pending and pending[0][0] <= n:
                    fin(pending.pop(0)[1])
            for _, t in pending:
                fin(t)

        def gate_mul(Gk, rb, sl):
            for c in range(4):
                def ev(s, bk, c=c):
                    tm = TMP[(2 * c + s) % 2]
                    act(tm.ap, bk.ap, AF.Silu, [bk.r()], [tm.r()])
                    tt(Gk.ap[:, c, slab(s)], Gk.ap[:, c, slab(s)], tm.ap, ALU.mult, [Gk.r(s), tm.r()], [Gk.r(s)])
                projT(rb, sl, c * 128, 128, HT, 8, ev)

        def run_layer(u, l):
            cond = u
            sample = (u == 1)
            koff = 256 if sample else 0
            nkt = 10 if sample else 8
            norm_phase(l, cond)
            if SUB <= 0:
                return

            G = GT[0]
            P.op("dve", lambda h: h.memset(VA1.ap[:, :, :, 64:65], 1.0), [], [VA1.r()])
            if sample:
                ds_ca = [new_dsem(), new_dsem(), new_dsem()]
                dma("sp", ds_ca[0], CSTA.ap, cnk[l].rearrange("(kt p) n -> p kt n", p=128), [], [CSTA.r()])
                for kt in range(2):
                    bk = next_bank()
                    for c in range(4):
                        tr(bk.ap[:, c * 128:(c + 1) * 128], CSTA.ap[:, kt, c * 128:(c + 1) * 128], [CSTA.r()], [bk.r()])
                    cp("dve", KA.ap[:, :, kt * 128:(kt + 1) * 128], bk.ap.rearrange("p (c n) -> p c n", c=4),
                       [bk.r()], [KA.r(None, 0)])
                dma("sp", ds_ca[0], CSTA.ap, cnv[l].rearrange("(kt p) n -> p kt n", p=128), [], [CSTA.r()])
                for kt in range(2):
                    cp("dve", VA1.ap[:, kt, :, 0:64], CSTA.ap[:, kt, :].rearrange("p (h d) -> p h d", h=8),
                       [CSTA.r()], [VA1.r(kt)])
                for h_ in range(8):
                    etab_head(l, h_)
            it = acquire(("q", u, l)); rb, sl = it.rb, it.sl
            for s in range(2):
                for c in range(4):
                    bk = next_bank()
                    for kc in range(8):
                        mm(bk.ap, sl[:, kc, c * 128:(c + 1) * 128], HT.ap[:, kc, slab(s)], kc == 0, kc == 7,
                           [rb.r(), HT.r(s, kc)], [bk.r()])
                    stt(QA.ap[:, c, slab(s)], bk.ap, HMASK.ap[:, 0:1], ONES512.ap, ALU.mult, ALU.mult,
                        [bk.r(), HMASK.r(), ONES512.r()], [QA.r(c, s)])
                    stt(QA1.ap[:, c, slab(s)], bk.ap, HMASK.ap[:, 8:9], ONES512.ap, ALU.mult, ALU.mult,
                        [bk.r(), HMASK.r(), ONES512.r()], [QA1.r(c, s)])
            it.release()
            if SUB == 1 and SUB2 <= 0:
                return
            it = acquire(("k", u, l)); rb, sl = it.rb, it.sl
            for c in range(4):
                def ev(s, bk, c=c):
                    cp(alt_eng(), KA.ap[:, c, koff + s * 512:koff + (s + 1) * 512], bk.ap, [bk.r()], [KA.r(c, 1 + s)])
                projT(rb, sl, c * 128, 128, HT, 8, ev)
            ds_o = [new_dsem() for _ in range(2)]
            if SUB == 1 and SUB2 <= 1:
                return
            if not sample:
                for t in range(8):
                    bk = next_bank()
                    for kc in range(8):
                        mm(bk.ap, HT.ap[:, kc, t * 128:(t + 1) * 128], sl[:, kc, :], kc == 0, kc == 7,
                           [rb.r(), HT.r(t // 4)], [bk.r()])
                    ost = OSTA[t % 2]
                    cp(alt_eng(), ost.ap, bk.ap, [bk.r()], [ost.r()])
                    out_dmas.append(dma("sp", ds_o[t % 2], sk_o[t // 2, l, (t % 2) * 128:(t % 2 + 1) * 128, :], ost.ap, [ost.r()], []))
            it.release()
            if SUB == 1 and SUB2 <= 2:
                return
            it = acquire(("v", u, l)); rb, sl = it.rb, it.sl
            for t in range(8):
                bk = next_bank()
                for kc in range(8):
                    mm(bk.ap, HT.ap[:, kc, t * 128:(t + 1) * 128], sl[:, kc, :], kc == 0, kc == 7,
                       [rb.r(), HT.r(t // 4)], [bk.r()])
                kt = t + (2 if sample else 0)
                cp(VENG, VA1.ap[:, kt, :, 0:64], bk.ap.rearrange("p (h d) -> p h d", h=8), [bk.r()], [VA1.r(kt)])
                if not sample:
                    ost = OSTA[t % 2]
                    cp("dve", ost.ap, bk.ap, [bk.r()], [ost.r()])
                    out_dmas.append(dma("sp", ds_o[t % 2], sv_o[t // 2, l, (t % 2) * 128:(t % 2 + 1) * 128, :], ost.ap, [ost.r()], []))

            it.release()
            if SUB <= 1:
                return
            if sample:
                def ktiles_a(t):
                    tl = [(0, "p", None), (1, "p", None)]
                    for (uu, quads) in _na_tiles(t):
                        tl.append((2 + uu, "m", (uu, quads)))
                    return tl
            else:
                def ktiles_a(t):
                    return [(2 * (t // 2), "p", None), (2 * (t // 2) + 1, "p", None)]
            for hg in range(2):
                def Qh(hh, hg=hg):
                    h_ = hg * 4 + hh
                    c = h_ // 2
                    qb = QA if h_ % 2 == 0 else QA1
                    return qb.ap[:, c, :], (lambda t, c=c, qb=qb: [qb.r(c, t // 4)])

                def Kh(hh, hg=hg):
                    h_ = hg * 4 + hh
                    c = h_ // 2
                    return KA.ap[:, c, :], [KA.r(c, None)]
                if sample:
                    def groups_a(t):
                        tl = ktiles_a(t)
                        return [tl[0:4], tl[4:]] if len(tl) > 4 else [tl]
                    attention(u, 0, l, G, Qh, Kh, VA1, PTA, YA, RDA, ktiles_a, 0.125, hg, etab=ETAB)
                else:
                    attention2(G, Qh, Kh, VA1, PTA4, YA, RDA, lambda t: [[2 * (t // 2), 2 * (t // 2) + 1]], 0.125, hg)
            it = acquire(("z", u, l)); rb, sl = it.rb, it.sl
            gate_mul(G, rb, sl)
            it.release()

            if SUB <= 2:
                return
            G = GT[1]
            ds_cc = [new_dsem() for _ in range(5)]
            if sample:
                dma("sp", ds_cc[0], COS.ap[64:96, :], rope_d[0], [], [COS.r()])
                dma("sp", ds_cc[0], SIN.ap[64:96, :], rope_d[1], [], [SIN.r()])
                dma("sp", ds_cc[1], CSTC.ap, cckv[l].rearrange("(kt p) n -> p kt n", p=128), [], [CSTC.r()])
                P.op("dve", lambda h: h.memset(CSTK.ap, 0.0), [], [CSTK.r()])
                dma("sp", ds_cc[2], CSTK.ap[:, :, 64:96], ckr[l].rearrange("(kt p) n -> p kt n", p=128), [], [CSTK.r()])
                krs = win_src(l, C_KR, 32)
                dma("pool", ds_cc[3], WKRP.ap[:, :, 0:16], krs[:, :, 16:32], [], [WKRP.r()])
                dma("pool", ds_cc[3], WKRP.ap[:, :, 16:32], krs[:, :, 0:16], [], [WKRP.r()])
                uqs = w_uq[l].rearrange("(kc p) (h d) -> p kc h d", p=128, h=8)
                wq4 = WUQP.ap.rearrange("p k (h d) -> p k h d", h=8)
                for kc in range(3):
                    dma("pool", ds_cc[4], wq4[:, kc, :, 0:16], uqs[:, kc, :, 80:96], [], [WUQP.r()])
                    dma("pool", ds_cc[4], wq4[:, kc, :, 16:32], uqs[:, kc, :, 64:80], [], [WUQP.r()])
            it = acquire(("su", u, l)); rb, sl = it.rb, it.sl
            for c in range(4):
                def ev(s, bk, c=c):
                    act(SU.ap[:, c, slab(s)], bk.ap, AF.Gelu_apprx_tanh, [bk.r()], [SU.r(s)])
                projT(rb, sl, c * 128, 128, HT, 8, ev)
            it.release()
            it = acquire(("sv", u, l)); rb, sl = it.rb, it.sl

            def sv_proj(t):
                bk = next_bank()
                for kc in range(8):
                    mm(bk.ap, HT.ap[:, kc, t * 128:(t + 1) * 128], sl[:, kc, :], kc == 0, kc == 7,
                       [rb.r(), HT.r(t // 4)], [bk.r()])
                gv = GV[t % 3]
                act(gv.ap, bk.ap, AF.Gelu_apprx_tanh, [bk.r()], [gv.r()])
                so = (t % 3) * 16
                st_ap = SMALL.ap[:, so:so + 6]
                mv_ap = SMALL.ap[:, so + 8:so + 10]
                rs_ap = SMALL.ap[:, so + 10:so + 11]
                sr = SMALL.r(t % 3)
                P.op("dve", lambda h, gv=gv, st_ap=st_ap: h.bn_stats(out=st_ap, in_=gv.ap), [gv.r()], [sr])
                P.op("dve", lambda h, st_ap=st_ap, mv_ap=mv_ap: h.bn_aggr(out=mv_ap, in_=st_ap), [sr], [sr])
                act(rs_ap, mv_ap[:, 1:2], AF.Sqrt, [sr], [sr], scale=1.0, bias=EPS)
                P.op("dve", lambda h, rs_ap=rs_ap: h.reciprocal(out=rs_ap, in_=rs_ap), [sr], [sr])
                ts(VB.ap[:, t, :], gv.ap, mv_ap[:, 0:1], rs_ap, ALU.subtract, ALU.mult, [gv.r(), sr], [VB.r(t)])

            def sv_sgu(t):
                bk2 = next_bank((4, 5))
                for g in range(4):
                    mm(bk2.ap[:, g * 128:(g + 1) * 128], VB.ap[:, t, g * 128:(g + 1) * 128], WST.ap[:, l, g, :],
                       True, True, [VB.r(t), WST.r()], [bk2.r()])
                tm = TMP[t % 2]
                tt(tm.ap, bk2.ap, BSB.ap[:, l, :], ALU.add, [bk2.r(), BSB.r()], [tm.r()])
                tt(G.ap[:, :, t * 128:(t + 1) * 128], tm.ap.rearrange("p (g n) -> p g n", g=4),
                   SU.ap[:, :, t * 128:(t + 1) * 128], ALU.mult, [tm.r(), SU.r(t // 4)], [G.r(t // 4)])

            sv_proj(0)
            sv_proj(1)
            for t in range(8):
                if t + 2 < 8:
                    sv_proj(t + 2)
                sv_sgu(t)
            it.release()
            it = acquire(("sz", u, l)); rb, sl = it.rb, it.sl
            gate_mul(G, rb, sl)
            it.release()

            if SUB <= 3:
                return
            G = GT[2]
            P.op("dve", lambda h: h.memset(V1C.ap[:, :, :, 64:65], 1.0), [], [V1C.r()])
            if sample:
                for kt in range(2):
                    bk = next_bank()
                    for c in range(2):
                        tr(bk.ap[:, c * 128:(c + 1) * 128], CSTC.ap[:, kt, c * 128:(c + 1) * 128], [CSTC.r()], [bk.r()])
                    cp("dve", CKVT.ap[:, :, kt * 128:(kt + 1) * 128], bk.ap[:, 0:256].rearrange("p (c n) -> p c n", c=2),
                       [bk.r()], [CKVT.r(0)])
                bk = next_bank()
                for kt in range(2):
                    tr(bk.ap[0:96, kt * 128:(kt + 1) * 128], CSTK.ap[:, kt, :], [CSTK.r()], [bk.r()])
                cp("act", KRT.ap[64:96, 0:256], bk.ap[64:96, 0:256], [bk.r()], [KRT.r(0)])
            it_dq = acquire(("dq", u, l))
            it = acquire(("dkvkr", u, l)); rb, sl = it.rb, it.sl

            def projA(s):
                for c in range(3):
                    bk = BANK[c]
                    for kc in range(8):
                        mm(bk.ap, it_dq.sl[:, kc, c * 128:(c + 1) * 128], HT.ap[:, kc, slab(s)], kc == 0, kc == 7,
                           [it_dq.rb.r(), HT.r(s)], [bk.r()])
                    act(SQN[c].ap, bk.ap, AF.Square, [bk.r()], [SQN[c].r()])

            def statsA(s):
                bkr = BANK[5]
                for c in range(3):
                    mm(bkr.ap, ONES.ap, SQN[c].ap, c == 0, c == 2, [ONES.r(), SQN[c].r()], [bkr.r()])
                rs = RS2[0]
                act(rs.ap, bkr.ap, AF.Ln, [bkr.r()], [rs.r()], scale=1.0 / 384, bias=EPS)
                act(rs.ap, rs.ap, AF.Exp, [rs.r()], [rs.r()], scale=-0.5)
                for c in range(3):
                    stt(DQN.ap[:, c, slab(s)], BANK[c].ap, QNT.ap[:, l, c:c + 1], rs.ap, ALU.mult, ALU.mult,
                        [BANK[c].r(), QNT.r(), rs.r()], [DQN.r(s)])

            def projB(s):
                for c in range(2):
                    bk = BANK[3 + c]
                    for kc in range(8):
                        mm(bk.ap, sl[:, kc, c * 128:(c + 1) * 128], HT.ap[:, kc, slab(s)], kc == 0, kc == 7,
                           [rb.r(), HT.r(s)], [bk.r()])
                    act(SQN[3 + c].ap, bk.ap, AF.Square, [bk.r()], [SQN[3 + c].r()])

            def statsB(s):
                bkr = BANK[5]
                for c in range(2):
                    mm(bkr.ap, ONES.ap, SQN[3 + c].ap, c == 0, c == 1, [ONES.r(), SQN[3 + c].r()], [bkr.r()])
                rs = RS2[1]
                act(rs.ap, bkr.ap, AF.Ln, [bkr.r()], [rs.r()], scale=1.0 / 256, bias=EPS)
                act(rs.ap, rs.ap, AF.Exp, [rs.r()], [rs.r()], scale=-0.5)
                for c in range(2):
                    stt(CKVT.ap[:, c, koff + s * 512:koff + (s + 1) * 512], BANK[3 + c].ap, KVNT.ap[:, l, c:c + 1], rs.ap,
                        ALU.mult, ALU.mult, [BANK[3 + c].r(), KVNT.r(), rs.r()], [CKVT.r(1 + s)])

            for s in range(2):
                projA(s)
                projB(s)
                statsA(s)
                statsB(s)
            it_dq.release()
            for s in range(2):
                bk = next_bank((6, 7))
                for kc in range(8):
                    mm(bk.ap[64:96, :], sl[:, kc, 256:288], HT.ap[:, kc, slab(s)], kc == 0, kc == 7, [rb.r(), HT.r(s)], [bk.r()])
                if not sample:
                    cp("act", KRT.ap[64:96, s * 512:(s + 1) * 512], bk.ap[64:96, :], [bk.r()], [KRT.r(1 + s)])
                else:
                    bk2 = next_bank((6, 7))
                    for kc in range(8):
                        mm(bk2.ap[64:96, :], WKRP.ap[:, kc, :], HT.ap[:, kc, slab(s)], kc == 0, kc == 7, [WKRP.r(), HT.r(s)], [bk2.r()])
                    t0, t1 = TMP[0], TMP[1]
                    tt(t0.ap[64:96, :], bk.ap[64:96, :], COS.ap[64:96, slab(s)], ALU.mult, [bk.r(), COS.r()], [t0.r()])
                    tt(t1.ap[64:96, :], bk2.ap[64:96, :], SIN.ap[64:96, slab(s)], ALU.mult, [bk2.r(), SIN.r()], [t1.r()])
                    tt(KRT.ap[64:96, koff + s * 512:koff + (s + 1) * 512], t0.ap[64:96, :], t1.ap[64:96, :], ALU.add,
                       [t0.r(), t1.r()], [KRT.r(1 + s)])
            if not sample:
                for t in range(8):
                    bk = next_bank()
                    for kc in range(8):
                        mm(bk.ap[:, 0:288], HT.ap[:, kc, t * 128:(t + 1) * 128], sl[:, kc, :], kc == 0, kc == 7,
                           [rb.r(), HT.r(t // 4)], [bk.r()])
                    so = 32 + (t % 2) * 16
                    st_ap = SMALL.ap[:, so:so + 6]
                    mv_ap = SMALL.ap[:, so + 8:so + 10]
                    rs_ap = SMALL.ap[:, so + 10:so + 11]
                    sr = SMALL.r(2 + t % 2)
                    P.op("dve", lambda h, bk=bk, st_ap=st_ap: h.bn_stats(out=st_ap, in_=bk.ap[:, 0:256]), [bk.r()], [sr])
                    P.op("dve", lambda h, st_ap=st_ap, mv_ap=mv_ap: h.bn_aggr(out=mv_ap, in_=st_ap), [sr], [sr])
                    stt(rs_ap, mv_ap[:, 0:1], mv_ap[:, 0:1], mv_ap[:, 1:2], ALU.mult, ALU.add, [sr], [sr])
                    act(rs_ap, rs_ap, AF.Sqrt, [sr], [sr], scale=1.0, bias=EPS)
                    P.op("dve", lambda h, rs_ap=rs_ap: h.reciprocal(out=rs_ap, in_=rs_ap), [sr], [sr])
                    ost = OSTC[t % 2]
                    stt(ost.ap[:, 0:256], bk.ap[:, 0:256], rs_ap, KVNB.ap[:, l, :], ALU.mult, ALU.mult,
                        [bk.r(), sr, KVNB.r()], [ost.r()])
                    cp("act", ost.ap[:, 256:288], bk.ap[:, 256:288], [bk.r()], [ost.r()])
                    rows = slice((t % 2) * 128, (t % 2 + 1) * 128)
                    out_dmas.append(dma("sp", ds_o[t % 2], sckv_o[t // 2, l, rows, :], ost.ap[:, 0:256], [ost.r()], []))
                    out_dmas.append(dma("sp", ds_o[t % 2], skr_o[t // 2, l, rows, :], ost.ap[:, 256:288], [ost.r()], []))
            it.release()
            itq = acquire(("wuq", u, l)); rbq, slq = itq.rb, itq.sl
            itk = acquire(("wukv", u, l)); rbk, slk = itk.rb, itk.sl
            vcols = slk.rearrange("p k (h e) -> p k h e", h=8)
            for kt in range(nkt):
                bk = next_bank()
                for kc in range(2):
                    mm(bk.ap, CKVT.ap[:, kc, kt * 128:(kt + 1) * 128], vcols[:, kc, :, 64:128], kc == 0, kc == 1,
                       [rbk.r(), CKVT.r(None)], [bk.r()])
                cp("dve", V1C.ap[:, kt, :, 0:64], bk.ap.rearrange("p (h d) -> p h d", h=8), [bk.r()], [V1C.r(kt)])
            nkeys = nkt * 128
            kslabs = [(0, 512), (512, 512)] + ([(1024, 256)] if sample else [])
            for hg in range(2):
                for hh in range(4):
                    h_ = hg * 4 + hh
                    for s in range(2):
                        bk = next_bank((0, 1))
                        for kc in range(3):
                            mm(bk.ap[0:96, :], slq[:, kc, h_ * 96:(h_ + 1) * 96], DQN.ap[:, kc, slab(s)], kc == 0, kc == 2,
                               [rbq.r(), DQN.r(s)], [bk.r()])
                        if not sample:
                            cp(alt_eng(), QC.ap[0:96, hh, slab(s)], bk.ap[0:96, :], [bk.r()], [QC.r(hh, s)])
                        else:
                            bk2 = next_bank((2, 3))
                            for kc in range(3):
                                mm(bk2.ap[64:96, :], WUQP.ap[:, kc, h_ * 32:(h_ + 1) * 32], DQN.ap[:, kc, slab(s)], kc == 0, kc == 2,
                                   [WUQP.r(), DQN.r(s)], [bk2.r()])
                            cp("act", QC.ap[0:64, hh, slab(s)], bk.ap[0:64, :], [bk.r()], [QC.r(hh, s)])
                            t0, t1 = TMP[0], TMP[1]
                            tt(t0.ap[64:96, :], bk.ap[64:96, :], COS.ap[64:96, slab(s)], ALU.mult, [bk.r(), COS.r()], [t0.r()])
                            tt(t1.ap[64:96, :], bk2.ap[64:96, :], SIN.ap[64:96, slab(s)], ALU.mult, [bk2.r(), SIN.r()], [t1.r()])
                            tt(QC.ap[64:96, hh, slab(s)], t0.ap[64:96, :], t1.ap[64:96, :], ALU.add, [t0.r(), t1.r()], [QC.r(hh, s)])
                    for (k0, kn) in kslabs:
                        bk = next_bank((0, 1))
                        for kc in range(2):
                            mm(bk.ap[0:64, 0:kn], slk[:, kc, h_ * 128:h_ * 128 + 64], CKVT.ap[:, kc, k0:k0 + kn], kc == 0, kc == 1,
                               [rbk.r(), CKVT.r(None)], [bk.r()])
                        cp(alt_eng(), KC.ap[0:64, hh, k0:k0 + kn], bk.ap[0:64, 0:kn], [bk.r()], [KC.r(hh, k0 // 512)])
                cp("dve", KC.ap[64:96, :, 0:nkeys], KRT.ap[64:96, 0:nkeys].unsqueeze(1).to_broadcast([32, 4, nkeys]),
                   [KRT.r(None)], [KC.r(None, None)])
                if sample:
                    def ktiles_c(t):
                        return [(kt, "p", None) for kt in range(10)]
                else:
                    def ktiles_c(t):
                        return [(2 * (t // 2), "p", None), (2 * (t // 2) + 1, "p", None)]

                def QhC(hh):
                    return QC.ap[0:96, hh, :], (lambda t, hh=hh: [QC.r(hh, t // 4)])

                def KhC(hh):
                    return KC.ap[0:96, hh, :], [KC.r(hh, None)]
                attention_c(u, l, G, QhC, KhC, ktiles_c, hg)
            itq.release()
            itk.release()
            it = acquire(("mz", u, l)); rb, sl = it.rb, it.sl
            gate_mul(G, rb, sl)
            it.release()

            if SUB <= 4:
                return
            if dbg is not None and (u, l) == unit_order[-1]:
                dsd = new_dsem()
                for k in range(3):
                    out_dmas.append(dma("pool", dsd, dbg["g"][k].rearrange("p (c n) -> p c n", c=4), GT[k].ap, [GT[k].r()], []))
                out_dmas.append(dma("pool", dsd, dbg["ht"].rearrange("p (c n) -> p c n", c=8), HT.ap, [HT.r()], []))
                out_dmas.append(dma("sp", new_dsem(), dbg["mod"], MODR.ap.rearrange("p l c j -> p (l c j)"), [MODR.r()], []))
            for half in range(2):
                for k in range(3):
                    if u == 1 and l == 0 and (1, 1) in unit_order:
                        etab_head(1, half * 3 + k)
                        if half == 1 and k == 2:
                            etab_head(1, 6)
                            etab_head(1, 7)
                    itg = acquire(("mg", u, l, k, half)); rbg, slg = itg.rb, itg.sl
                    itb = acquire(("wb", u, l, k, half)); rbb, slb = itb.rb, itb.sl
                    for cc in range(4):
                        c = half * 4 + cc
                        for s in range(2):
                            bkB = next_bank((0, 1, 4, 5))
                            for kc in range(4):
                                mm(bkB.ap, slb[:, kc, cc * 128:(cc + 1) * 128], GT[k].ap[:, kc, slab(s)], kc == 0, kc == 3,
                                   [rbb.r(), GT[k].r(s)], [bkB.r()])
                            bkG = next_bank((2, 3, 6, 7))
                            for kc in range(8):
                                mm(bkG.ap, slg[:, kc, cc * 128:(cc + 1) * 128], HT.ap[:, kc, slab(s)], kc == 0, kc == 7,
                                   [rbg.r(), HT.r(s)], [bkG.r()])
                            sg = SG[(cc * 2 + s) % 2]
                            act(sg.ap, bkG.ap, AF.Sigmoid, [bkG.r()], [sg.r()])
                            if k == 0:
                                tt(ACC.ap[:, cc, slab(s)], bkB.ap, sg.ap, ALU.mult, [bkB.r(), sg.r()], [ACC.r(cc, s)])
                            else:
                                tm = TMP[(cc * 2 + s) % 2]
                                tt(tm.ap, bkB.ap, sg.ap, ALU.mult, [bkB.r(), sg.r()], [tm.r()])
                                if k == 1:
                                    tt(ACC.ap[:, cc, slab(s)], ACC.ap[:, cc, slab(s)], tm.ap, ALU.add, [ACC.r(cc, s), tm.r()], [ACC.r(cc, s)])
                                else:
                                    tt(MT.ap[:, c, slab(s)], ACC.ap[:, cc, slab(s)], tm.ap, ALU.add, [ACC.r(cc, s), tm.r()], [MT.r(c, s)])
                    itg.release()
                    itb.release()
            if dbg is not None and (u, l) == unit_order[-1]:
                out_dmas.append(dma("pool", new_dsem(), dbg["mt"].rearrange("p (c n) -> p c n", c=8), MT.ap, [MT.r()], []))
            itos = [acquire(("wo", u, l, half)) for half in range(2)]
            for s in range(2):
                for c in range(8):
                    ito = itos[c // 4]
                    rbo, slo = ito.rb, ito.sl
                    cc = c % 4
                    bk = next_bank()
                    for kc in range(8):
                        mm(bk.ap, slo[:, kc, cc * 128:(cc + 1) * 128], MT.ap[:, kc, slab(s)], kc == 0, kc == 7,
                           [rbo.r(), MT.r(None, s)], [bk.r()])
                    stt(XT.ap[:, c, slab(s)], bk.ap, modG(l, c, cond), XT.ap[:, c, slab(s)], ALU.mult, ALU.add,
                        [bk.r(), MODR.r(), XT.r(c, s)], [XT.r(c, s)])
            for ito in itos:
                ito.release()

        def attention_c(u, l, G, QhC, KhC, ktiles_c, hg):
            if u == 0:
                attention2(G, QhC, KhC, V1C, PTC4, YC, RDC, lambda t: [[2 * (t // 2), 2 * (t // 2) + 1]], 96.0 ** -0.5, hg)
            else:
                attention2(G, QhC, KhC, V1C, PTC4, YC, RDC, lambda t: [[0, 1, 2, 3], [4, 5, 6, 7], [8, 9]], 96.0 ** -0.5, hg)

        def attention_long(u, l, G, Qh, Kh, hg):
            scale = 96.0 ** -0.5
            steps = [(t, hh, gi) for t in range(8) for hh in range(4) for gi in range(2)]
            obank = BANK[0]
            tbank = BANK[1]

            def do_S(n):
                t, hh, gi = steps[n]
                sb_ = SBUF2[n % 3]
                qap, qregs = Qh(hh)
                kap, kregs = Kh(hh)
                for j in range(5):
                    kt = gi * 5 + j
                    mm(sb_.ap[:, j * 128:(j + 1) * 128], kap[:, kt * 128:(kt + 1) * 128], qap[:, t * 128:(t + 1) * 128],
                       True, True, kregs + qregs(t), [sb_.r()])
                pt = PTC[n % 3]
                act(pt.ap[:, 0:640], sb_.ap[:, 0:640], AF.Exp, [sb_.r()], [pt.r()], scale=scale)

            def do_PV(n):
                t, hh, gi = steps[n]
                h_ = hg * 4 + hh
                pt = PTC[n % 3]
                for j in range(5):
                    kt = gi * 5 + j
                    mm(obank.ap[:, hh * 65:(hh + 1) * 65], pt.ap[:, j * 128:(j + 1) * 128], V1C.ap[:, kt, h_, 0:65],
                       kt == 0, kt == 9, [pt.r(), V1C.r(kt)], [obank.r()])
                if hh == 3 and gi == 1:
                    y = YC[t % 2]
                    rd = RDC[t % 2]
                    o3 = obank.ap[:, 0:260].rearrange("p (h d) -> p h d", h=4)
                    P.op("dve", lambda h: h.reciprocal(out=rd.ap, in_=o3[:, :, 64]), [obank.r()], [rd.r()])
                    tt(y.ap.rearrange("p (h d) -> p h d", h=4), o3[:, :, 0:64], rd.ap.unsqueeze(2).to_broadcast([128, 4, 64]),
                       ALU.mult, [obank.r(), rd.r()], [y.r()])
                    for j in range(2):
                        tr(tbank.ap[:, j * 128:(j + 1) * 128], y.ap[:, j * 128:(j + 1) * 128], [y.r()], [tbank.r()])
                    for j in range(2):
                        cp("act", G.ap[:, hg * 2 + j, t * 128:(t + 1) * 128], tbank.ap[:, j * 128:(j + 1) * 128],
                           [tbank.r()], [G.r(t // 4)])

            do_S(0)
            do_S(1)
            for n in range(len(steps)):
                if n + 2 < len(steps):
                    do_S(n + 2)
                do_PV(n)

        for u in range(2):
            load_unit(u)
            if u == 0:
                mod_phase(0)
                if not interleave_mod1:
                    mod_phase(1)
            for l in range(2):
                if (u, l) in unit_order:
                    run_layer(u, l)
            if u == 0:
                prefetch_x(1)
            final_unit(u)

        fin = P.op("sp", lambda h: h.nop())
        fin.deps.update(out_dmas)
        stats = P.emit(block, sems)
    return nc, stats


def _static_tables():
    ident = np.eye(128, dtype=np.float32)
    pos = np.arange(1024)
    row = (pos // 64).astype(np.float32)
    col = (pos % 64).astype(np.float32)
    inv = (np.float32(10000.0) ** (-np.arange(8, dtype=np.float32) / np.float32(8))).astype(np.float32)
    ang = np.concatenate([row[:, None] * inv, col[:, None] * inv], axis=-1).astype(np.float32)
    cos, sin = np.cos(ang).astype(np.float32), np.sin(ang).astype(np.float32)
    cos2 = np.concatenate([cos, cos], axis=1).T
    sins = np.concatenate([-sin, sin], axis=1).T
    rope = np.ascontiguousarray(np.stack([cos2, sins], 0)).astype(np.float32)
    kc = np.arange(64)[:, None]
    c = np.arange(64)[None, :]
    lo = np.clip(c - 8, 0, 48)
    m = ((kc >= lo) & (kc < lo + 16)).astype(np.float32)
    cmask = np.ascontiguousarray(np.concatenate([m, m], 0))
    return ident, rope, cmask


_CACHE = {}


def kernel(x_prompt, x_sample, cache_na_k, cache_na_v, cache_mla_ckv, cache_mla_krope, c, c_ctx,
           norm_g, w_mod, b_mod, w_in, na_rpb, sgu_w, sgu_b, mla_q_norm, mla_w_uq, mla_kv_norm,
           mla_w_ukv, w_branch, w_out, final_norm_g, _stage=99):
    f = lambda a: np.ascontiguousarray(np.asarray(a, dtype=np.float32))
    if _stage not in _CACHE:
        _CACHE[_stage] = build_program(_stage)
    nc, stats = _CACHE[_stage]
    ident, rope, cmask = _static_tables()
    x_prompt, x_sample = f(x_prompt), f(x_sample)
    rpbpad = np.zeros((2, 8, 15, 128), np.float32)
    rpbpad[..., 48:79] = f(na_rpb)
    shared = {
        "norm_g": f(norm_g), "w_mod": f(w_mod), "b_mod": f(b_mod), "w_in": f(w_in), "rpbpad": rpbpad,
        "sgu_w": f(sgu_w), "sgu_b": f(sgu_b).reshape(2, 512), "q_norm": f(mla_q_norm), "w_uq": f(mla_w_uq),
        "kv_norm": f(mla_kv_norm), "w_ukv": f(mla_w_ukv), "w_branch": f(w_branch), "w_out": f(w_out),
        "fng": f(final_norm_g), "ident": ident, "rope": rope, "cmask": cmask,
    }
    c, c_ctx = f(c), f(c_ctx)
    cnk, cnv = f(cache_na_k).reshape(8, 2, 256, 512), f(cache_na_v).reshape(8, 2, 256, 512)
    cckv, ckr = f(cache_mla_ckv), f(cache_mla_krope)
    in_maps = []
    for i in range(NCORES):
        m = dict(shared)
        m["xp"] = x_prompt[4 * i:4 * i + 4].reshape(1024, 1024)
        m["xs"] = x_sample[i]
        m["cnk"], m["cnv"], m["cckv"], m["ckr"] = cnk[i], cnv[i], cckv[i], ckr[i]
        m["c2"] = np.ascontiguousarray(np.stack([c_ctx, c[i]], 0))
        in_maps.append(m)
    res = run_bass_kernel_spmd(nc, in_maps, core_ids=list(range(NCORES)))
    rs = res.results
    global _LAST
    _LAST = rs
    y_prompt = np.concatenate([r["yp"].reshape(4, 256, 1024) for r in rs], 0)
    y_sample = np.stack([r["ys"] for r in rs], 0)
    sk = np.concatenate([r["sk"].reshape(4, 2, 256, 8, 64) for r in rs], 0)
    sv = np.concatenate([r["sv"].reshape(4, 2, 256, 8, 64) for r in rs], 0)
    sckv = np.concatenate([r["sckv"] for r in rs], 0)
    skr = np.concatenate([r["skr"] for r in rs], 0)
    return (y_prompt.astype(np.float32), y_sample.astype(np.float32), sk.astype(np.float32), sv.astype(np.float32),
            sckv.astype(np.float32), skr.astype(np.float32))
```
